# Optimizing a Trainium2 kernel written in Bass

```python
import math
import jax, jax.numpy as jnp
from jax import lax
import numpy as np

D_MODEL = 2048
BATCH = 16
SEQ = 2048
DEPTH = 2

D_PLE = 256
D_MIX = D_MODEL
D_SGU = D_MIX // 2
D_SSM = D_MIX - D_SGU
SGU_CHUNK = 128
SGU_HEAD = 128
SGU_HEADS = D_SGU // SGU_HEAD
SSM_GROUP = 16
SSM_GROUPS = D_SSM // SSM_GROUP
SSM_STATE = 64
D_FF = 5632
CONV_W = 3
EPS = 1e-6
DT_MIN = 1e-3
DT_MAX = 1e-1

kernel_name = "hybrid_sgu_s5_convffn_ple"


def rmsnorm(x, g):
    x32 = x.astype(jnp.float32)
    y = x32 * lax.rsqrt(jnp.mean(x32 * x32, axis=-1, keepdims=True) + EPS)
    return (y * g.astype(jnp.float32)).astype(x.dtype)


def layernorm(x, g):
    x32 = x.astype(jnp.float32)
    xc = x32 - jnp.mean(x32, axis=-1, keepdims=True)
    y = xc * lax.rsqrt(jnp.mean(xc * xc, axis=-1, keepdims=True) + EPS)
    return (y * g.astype(jnp.float32)).astype(x.dtype)


def spatial_gating(u, v, g, w_s, b_s):
    bsz, seq, _ = u.shape
    v = layernorm(v, g)
    v = v.reshape(bsz, seq // SGU_CHUNK, SGU_CHUNK, SGU_HEADS, SGU_HEAD)
    mask = jnp.tril(jnp.ones((SGU_CHUNK, SGU_CHUNK), dtype=w_s.dtype))
    v = jnp.einsum('hts,bnshc->bnthc', w_s * mask[None], v)
    v = v + jnp.transpose(b_s)[None, None, :, :, None]
    return u * v.reshape(bsz, seq, D_SGU)


def cmul(ar, ai, br, bi):
    return ar * br - ai * bi, ar * bi + ai * br


def s5_combine(left, right):
    al_r, al_i, bl_r, bl_i = left
    ar_r, ar_i, br_r, br_i = right
    a_r, a_i = cmul(ar_r, ar_i, al_r, al_i)
    ab_r, ab_i = cmul(ar_r, ar_i, bl_r, bl_i)
    return a_r, a_i, ab_r + br_r, ab_i + br_i


def s5_mixer(xs, lam_re, lam_im, log_dt, b_re, b_im, c_re, c_im, d, glu_w, glu_b):
    f32 = jnp.float32
    dtype = xs.dtype
    bsz, seq, _ = xs.shape
    x32 = xs.astype(f32)
    xg = x32.reshape(bsz, seq, SSM_GROUPS, SSM_GROUP)
    lr = lam_re.astype(f32)
    li = lam_im.astype(f32)
    dt = jnp.exp(log_dt.astype(f32))[:, None]
    mag = jnp.exp(lr * dt)
    ab_r = mag * jnp.cos(li * dt)
    ab_i = mag * jnp.sin(li * dt)
    den = lr * lr + li * li
    nr = ab_r - 1.0
    q_r = (nr * lr + ab_i * li) / den
    q_i = (ab_i * lr - nr * li) / den
    b_re32 = b_re.astype(f32)
    b_im32 = b_im.astype(f32)
    bb_r = q_r[..., None] * b_re32 - q_i[..., None] * b_im32
    bb_i = q_r[..., None] * b_im32 + q_i[..., None] * b_re32
    bu_r = jnp.einsum('gnc,bsgc->bsgn', bb_r, xg)
    bu_i = jnp.einsum('gnc,bsgc->bsgn', bb_i, xg)
    a_r = jnp.broadcast_to(ab_r, bu_r.shape)
    a_i = jnp.broadcast_to(ab_i, bu_i.shape)
    _, _, s_r, s_i = lax.associative_scan(s5_combine, (a_r, a_i, bu_r, bu_i), axis=1)
    y = (jnp.einsum('gcn,bsgn->bsgc', c_re.astype(f32), s_r)
         - jnp.einsum('gcn,bsgn->bsgc', c_im.astype(f32), s_i))
    y = y.reshape(bsz, seq, D_SSM) + d.astype(f32) * x32
    z = jax.nn.gelu(y)
    z = z * jax.nn.sigmoid(z @ glu_w.astype(f32) + glu_b.astype(f32))
    return z.astype(dtype)


def conv_ffn(h, w_up, conv_w, conv_b, w_down):
    a = h @ w_up
    seq = a.shape[1]
    ap = jnp.pad(a, ((0, 0), (CONV_W - 1, 0), (0, 0)))
    c = (conv_w[0] * ap[:, 0:seq] + conv_w[1] * ap[:, 1:seq + 1]
         + conv_w[2] * ap[:, 2:seq + 2] + conv_b)
    gate, up = jnp.split(c, 2, axis=-1)
    return (jax.nn.silu(gate) * up) @ w_down


def setup_inputs(seed: int = 0) -> dict:
    key = jax.random.key(seed)
    ks = iter(jax.random.split(key, 40))
    nrm = lambda shape, s: jax.random.normal(next(ks), shape, jnp.float32) * s
    gain = lambda shape: 1.0 + nrm(shape, 0.05)
    x = jax.random.normal(next(ks), (BATCH, SEQ, D_MODEL), jnp.float32)
    p = jax.random.normal(next(ks), (DEPTH, BATCH, SEQ, D_PLE), jnp.float32)
    n_idx = jnp.arange(SSM_STATE, dtype=jnp.float32)
    lam_re = -0.5 + nrm((DEPTH, SSM_GROUPS, SSM_STATE), 0.01)
    lam_im = math.pi * n_idx + nrm((DEPTH, SSM_GROUPS, SSM_STATE), 0.01)
    log_dt = jax.random.uniform(next(ks), (DEPTH, SSM_GROUPS), jnp.float32,
                                math.log(DT_MIN), math.log(DT_MAX))
    return {
        "x": x,
        "p": p,
        "mix_norm": gain((DEPTH, D_MODEL)),
        "w_in": nrm((DEPTH, D_MODEL, D_SGU * 2 + D_SSM), D_MODEL ** -0.5),
        "sgu_norm": gain((DEPTH, D_SGU)),
        "sgu_w": nrm((DEPTH, SGU_HEADS, SGU_CHUNK, SGU_CHUNK), SGU_CHUNK ** -0.5),
        "sgu_b": 1.0 + nrm((DEPTH, SGU_HEADS, SGU_CHUNK), 0.1),
        "s5_lam_re": lam_re,
        "s5_lam_im": lam_im,
        "s5_log_dt": log_dt,
        "s5_b_re": nrm((DEPTH, SSM_GROUPS, SSM_STATE, SSM_GROUP), (2 * SSM_GROUP) ** -0.5),
        "s5_b_im": nrm((DEPTH, SSM_GROUPS, SSM_STATE, SSM_GROUP), (2 * SSM_GROUP) ** -0.5),
        "s5_c_re": nrm((DEPTH, SSM_GROUPS, SSM_GROUP, SSM_STATE), SSM_STATE ** -0.5),
        "s5_c_im": nrm((DEPTH, SSM_GROUPS, SSM_GROUP, SSM_STATE), SSM_STATE ** -0.5),
        "s5_d": nrm((DEPTH, D_SSM), 1.0),
        "s5_glu_w": nrm((DEPTH, D_SSM, D_SSM), D_SSM ** -0.5),
        "s5_glu_b": nrm((DEPTH, D_SSM), 0.02),
        "out_norm_a": gain((DEPTH, D_SGU)),
        "out_norm_b": gain((DEPTH, D_SSM)),
        "w_out": nrm((DEPTH, D_MIX, D_MODEL), D_MIX ** -0.5),
        "ffn_norm": gain((DEPTH, D_MODEL)),
        "ffn_w_up": nrm((DEPTH, D_MODEL, 2 * D_FF), D_MODEL ** -0.5),
        "ffn_conv_w": nrm((DEPTH, CONV_W, 2 * D_FF), CONV_W ** -0.5),
        "ffn_conv_b": nrm((DEPTH, 2 * D_FF), 0.02),
        "ffn_w_down": nrm((DEPTH, D_FF, D_MODEL), D_FF ** -0.5),
        "ple_norm": gain((DEPTH, D_MODEL)),
        "ple_w_gate": nrm((DEPTH, D_MODEL, D_MODEL), D_MODEL ** -0.5),
        "ple_w_proj": nrm((DEPTH, D_PLE, D_MODEL), D_PLE ** -0.5),
        "final_norm": gain((D_MODEL,)),
    }


def reference(x, p, mix_norm, w_in, sgu_norm, sgu_w, sgu_b, s5_lam_re, s5_lam_im, s5_log_dt,
              s5_b_re, s5_b_im, s5_c_re, s5_c_im, s5_d, s5_glu_w, s5_glu_b,
              out_norm_a, out_norm_b, w_out, ffn_norm, ffn_w_up, ffn_conv_w, ffn_conv_b,
              ffn_w_down, ple_norm, ple_w_gate, ple_w_proj, final_norm):
    for i in range(DEPTH):
        h = rmsnorm(x, mix_norm[i])
        z = h @ w_in[i]
        u = jax.nn.gelu(z[..., :D_SGU])
        v = jax.nn.gelu(z[..., D_SGU:2 * D_SGU])
        xs = z[..., 2 * D_SGU:]
        ya = spatial_gating(u, v, sgu_norm[i], sgu_w[i], sgu_b[i])
        yb = s5_mixer(xs, s5_lam_re[i], s5_lam_im[i], s5_log_dt[i], s5_b_re[i], s5_b_im[i],
                      s5_c_re[i], s5_c_im[i], s5_d[i], s5_glu_w[i], s5_glu_b[i])
        mix = jnp.concatenate([rmsnorm(ya, out_norm_a[i]), rmsnorm(yb, out_norm_b[i])], axis=-1)
        x = x + mix @ w_out[i]
        x = x + conv_ffn(rmsnorm(x, ffn_norm[i]), ffn_w_up[i], ffn_conv_w[i], ffn_conv_b[i], ffn_w_down[i])
        gate = jax.nn.sigmoid(rmsnorm(x, ple_norm[i]) @ ple_w_gate[i])
        x = x + gate * (p[i] @ ple_w_proj[i])
    return rmsnorm(x, final_norm)
```

```python
import numpy as np
import concourse.bass as bass
import concourse.mybir as mybir
from concourse.bass_utils import run_bass_kernel_spmd

F32 = mybir.dt.float32
BF16 = mybir.dt.bfloat16
I32 = mybir.dt.int32
AF = mybir.ActivationFunctionType
ALU = mybir.AluOpType

D = 2048
SEQ = 2048
NB = 16
DEPTH = 2
D_PLE = 256
D_SGU = 1024
D_SSM = 1024
D_FF = 5632
NG = 64
EPS = 1e-6
NT = 512
NCH = NT // 128
KT_D = D // 128
TWO_PI = float(2 * np.pi)
SHR = 1.0 - 1e-6

ENGS = ("pe", "act", "dve", "pool", "sp")


class _Op:
    __slots__ = ("eng", "fn", "needs_inc", "waits", "dma", "count", "small")


class Prog:
    def __init__(self):
        self.ops = {e: [] for e in ENGS}
        self.lastw = {}
        self.readers = {}
        self.dma_keys = {}
        self.small = False

    def add(self, eng, fn, reads=(), writes=(), dma=None):
        op = _Op()
        op.eng = eng; op.fn = fn; op.needs_inc = False; op.dma = None; op.count = 0
        op.small = self.small
        deps = []
        for k in reads:
            t = self.lastw.get(k)
            if t is not None:
                deps.append(t)
        for k in writes:
            t = self.lastw.get(k)
            if t is not None:
                deps.append(t)
            deps.extend(self.readers.get(k, {}).values())
        if dma is not None:
            ent = self.dma_keys.setdefault(dma, [len(self.dma_keys), 0, None])
            if ent[2] is not None:
                deps.append(ent[2])
            ent[1] += 1
            tok = ("dma", ent[0], 16 * ent[1])
            ent[2] = tok
            op.dma = ent[0]
        else:
            tok = ("eng", eng, op)
        w = []
        for d in deps:
            if d[0] == "eng":
                if d[1] == eng and dma is None and not d[2].small:
                    continue
                d[2].needs_inc = True
            w.append(d)
        op.waits = w
        self.ops[eng].append(op)
        for k in writes:
            self.lastw[k] = tok
            self.readers[k] = {}
        for k in reads:
            rk = (tok[0], tok[1]) if tok[0] == "eng" else tok
            self.readers.setdefault(k, {})[rk] = tok
        return tok

    def finalize_counts(self):
        for e in ENGS:
            c = 0
            for op in self.ops[e]:
                if op.needs_inc:
                    c += 1
                    op.count = c

    def emit_engine(self, eng, engine, eng_sems, dma_sems):
        waited = {}
        for op in self.ops[eng]:
            for d in op.waits:
                if d[0] == "eng":
                    key = ("e", d[1]); sem = eng_sems[d[1]]; val = d[2].count
                else:
                    key = ("d", d[1]); sem = dma_sems[d[1]]; val = d[2]
                if waited.get(key, 0) >= val:
                    continue
                waited[key] = val
                engine.wait_ge(sem, val)
            ins = op.fn()
            if ins is None:
                continue
            if op.dma is not None:
                ins.then_inc(dma_sems[op.dma], 16)
            elif op.needs_inc:
                ins.then_inc(eng_sems[eng], 1)


def _weight_specs():
    return [
        ("w_in", D, 3 * 1024),
        ("s5_glu_w", D_SSM, D_SSM),
        ("w_out", D, D),
        ("ffn_w_up", D, 2 * D_FF),
        ("ffn_w_down", D_FF, D),
        ("ple_w_gate", D, D),
        ("ple_w_proj", D_PLE, D),
    ]


def build_program(n_tiles_per_seq=SEQ // NT, n_seq=2, depth=DEPTH, debug=None):
    nc = bass.Bass("TRN2", target_bir_lowering=False)
    P = Prog()
    debug = debug or set()
    L = depth

    def dram_in(name, shape):
        return nc.dram_tensor(name, list(shape), F32, kind="ExternalInput").ap()

    x_d = dram_in("x", [n_seq, SEQ, D])
    p_d = dram_in("p", [DEPTH, n_seq, SEQ, D_PLE])
    mix_norm_d = dram_in("mix_norm", [DEPTH, D])
    w_in_d = dram_in("w_in", [DEPTH, D, 3072])
    sgu_norm_d = dram_in("sgu_norm", [DEPTH, D_SGU])
    sgu_w_d = dram_in("sgu_w", [DEPTH, 8, 128, 128])
    sgu_b_d = dram_in("sgu_b", [DEPTH, 8, 128])
    lam_re_d = dram_in("s5_lam_re", [DEPTH, NG, 64])
    lam_im_d = dram_in("s5_lam_im", [DEPTH, NG, 64])
    log_dt_d = dram_in("s5_log_dt", [DEPTH, NG])
    b_re_d = dram_in("s5_b_re", [DEPTH, NG, 64, 16])
    b_im_d = dram_in("s5_b_im", [DEPTH, NG, 64, 16])
    c_re_d = dram_in("s5_c_re", [DEPTH, NG, 16, 64])
    c_im_d = dram_in("s5_c_im", [DEPTH, NG, 16, 64])
    s5_d_d = dram_in("s5_d", [DEPTH, D_SSM])
    glu_w_d = dram_in("s5_glu_w", [DEPTH, D_SSM, D_SSM])
    glu_b_d = dram_in("s5_glu_b", [DEPTH, D_SSM])
    out_norm_a_d = dram_in("out_norm_a", [DEPTH, D_SGU])
    out_norm_b_d = dram_in("out_norm_b", [DEPTH, D_SSM])
    w_out_d = dram_in("w_out", [DEPTH, D, D])
    ffn_norm_d = dram_in("ffn_norm", [DEPTH, D])
    ffn_up_d = dram_in("ffn_w_up", [DEPTH, D, 2 * D_FF])
    conv_w_d = dram_in("ffn_conv_w", [DEPTH, 3, 2 * D_FF])
    conv_b_d = dram_in("ffn_conv_b", [DEPTH, 2 * D_FF])
    ffn_down_d = dram_in("ffn_w_down", [DEPTH, D_FF, D])
    ple_norm_d = dram_in("ple_norm", [DEPTH, D])
    ple_gate_d = dram_in("ple_w_gate", [DEPTH, D, D])
    ple_proj_d = dram_in("ple_w_proj", [DEPTH, D_PLE, D])
    final_norm_d = dram_in("final_norm", [D])
    wsrc = {"w_in": w_in_d, "s5_glu_w": glu_w_d, "w_out": w_out_d, "ffn_w_up": ffn_up_d,
            "ffn_w_down": ffn_down_d, "ple_w_gate": ple_gate_d, "ple_w_proj": ple_proj_d}

    out_d = nc.dram_tensor("out", [n_seq, SEQ, D], F32, kind="ExternalOutput").ap()

    wscr = {}
    for l in range(L):
        for name, K, M in _weight_specs():
            wscr[(name, l)] = nc.dram_tensor(f"ws_{name}_{l}", [M // 128, 128, (K // 128) * 128], BF16).ap()
    dkind = "ExternalOutput" if "dump" in (debug or ()) else "Internal"
    s5w_scr = nc.dram_tensor("s5w_scr", [L, NG, 128, 4 * 128], BF16, kind=dkind).ap()
    tab_scr = nc.dram_tensor("tab_scr", [L, NG, 128, 2, SEQ], BF16, kind=dkind).ap()

    ARENA_BYTES = 212800
    arena_cm = nc.sbuf_tensor("arena", [128, ARENA_BYTES // 4], F32)
    arena = arena_cm.__enter__()
    cur = [0]

    def carve_at(off, nbytes, dtype, pattern=None, **kw):
        assert off % 4 == 0 and nbytes % 4 == 0
        assert off + nbytes <= ARENA_BYTES, (off, nbytes)
        v = arena[:, off // 4:(off + nbytes) // 4]
        if dtype != F32:
            v = v.bitcast(dtype)
        if pattern:
            v = v.rearrange(pattern, **kw)
        return v

    def carve(nbytes, dtype, pattern=None, **kw):
        off = cur[0]
        cur[0] += (nbytes + 31) // 32 * 32
        return carve_at(off, nbytes, dtype, pattern, **kw)

    ident = carve(512, F32)
    ones_b = carve(256, BF16)
    rows_b = carve(256 + 2 * 2048, BF16)
    ccol = carve(64, F32)
    maskcol = carve(32, F32)
    NVEC = 80
    gvec = carve(L * NVEC * 4, F32, "p (l v) -> p l v", l=L)
    fin_g = carve(64, F32)
    convc = carve(L * 4 * 88 * 4, F32, "p (l a j) -> p l a j", l=L, a=4)
    sgu_g = carve(L * 2048, BF16, "p (l c) -> p l c", l=L)
    wmT = carve(L * 2048, BF16, "p (l h t) -> p l h t", l=L, h=8)
    amag = carve(L * 256, F32, "p (l g) -> p l g", l=L)
    cc = carve(L * 88 * 2 * 4, F32, "p (l j t) -> p l j t", l=L, t=2)
    s5c = carve(L * 256, F32, "p (l g) -> p l g", l=L)
    const_end = cur[0]

    xT = carve(32768, F32, "p (k n) -> p k n", n=NT)
    hT = carve(16384, BF16, "p (k n) -> p k n", n=NT)
    mixT = carve(16384, BF16, "p (k n) -> p k n", n=NT)
    tmpf = [carve(2048, F32) for _ in range(4)]
    sqb = [carve(1024, BF16) for _ in range(2)]
    rstd_x = carve(2048, F32)
    rstd_a = carve(2048, F32)
    rstd_b = carve(2048, F32)
    smallc = carve(128, F32)
    NWB = 5
    WB_BYTES = 22 * 256
    wbuf = [carve(WB_BYTES, BF16) for _ in range(NWB)]
    NTB = 4
    gring = [carve(3072, BF16) for _ in range(NTB)]
    U0 = cur[0]
    UNION_BYTES = 69632
    cur[0] += UNION_BYTES
    main_end = cur[0]
    assert main_end <= ARENA_BYTES, main_end
    uT = carve_at(U0 + 0, 8192, BF16, "p (k n) -> p k n", n=NT)
    vtok = carve_at(U0 + 8192, 8192, BF16, "p (c d) -> p c d", c=NCH)
    xsb = carve_at(U0 + 16384, 8192, BF16, "p (k n) -> p k n", n=NT)
    zf = carve_at(U0 + 24576, 16384, F32, "p (k n) -> p k n", n=NT)
    zb = carve_at(U0 + 40960, 8192, BF16, "p (k n) -> p k n", n=NT)
    NS5 = 3
    S5B = U0 + 49152
    s5pb = [carve_at(S5B + i * 2048, 2048, BF16) for i in range(2)]
    s5t12 = [carve_at(S5B + 4096 + i * 2048, 2048, BF16) for i in range(2)]
    s5g = [carve_at(S5B + 8192 + i * 2048, 2048, F32) for i in range(NS5)]
    s5g12 = [carve_at(S5B + 8192 + NS5 * 2048 + i * 2048, 2048, BF16, "p (a n) -> p a n", a=2) for i in range(NS5)]
    assert S5B + 8192 + NS5 * 4096 <= U0 + UNION_BYTES
    hid = carve_at(U0 + 0, 45056, BF16, "p (k n) -> p k n", n=NT)
    abuf = [carve_at(U0 + 45056 + i * 2064, 2056, F32) for i in range(4)]
    ptok = carve_at(U0 + 53312, 4096, F32, "p (c d) -> p c d", c=NCH)
    pT = carve_at(U0 + 57408, 2048, BF16, "p (k n) -> p k n", n=NT)
    iostage = [carve_at(U0 + i * 8192, 8192, F32) for i in range(2)]
    UNION_KEYS_MIX = (["uT", "vtok", "xsb", "zf", "zb", "s5pb0", "s5pb1", "s5t120", "s5t121"]
                      + [f"s5g{i}" for i in range(3)] + [f"s5g12{i}" for i in range(3)])
    UNION_KEYS_FFN = ["hid", "ab0", "ab1", "ab2", "ab3", "ptok", "pT", "io0", "io1"]
    mix_first = {}

    def mixw(l_, s_, ti_, key):
        k = (s_, ti_, l_, key)
        if k in mix_first:
            return []
        mix_first[k] = True
        return UNION_KEYS_FFN

    PR0 = const_end
    pro = {}
    pcur = [PR0]

    def pcarve(nbytes, dtype, pattern=None, **kw):
        off = pcur[0]
        pcur[0] += (nbytes + 31) // 32 * 32
        return carve_at(off, nbytes, dtype, pattern, **kw)

    ps_cms = [nc.psum_tensor("psall", [128, 8 * 512], F32)]
    psall = ps_cms[0].__enter__()
    ps = [psall[:, i * 512:(i + 1) * 512] for i in range(8)]
    bank_rr = {"mm": [0, 1, 2, 3], "acc": [6], "aux": [6, 7], "stat": [7], "s5p": [0, 1, 2, 3], "mm2": [4, 5]}
    bank_ctr = {k_: 0 for k_ in bank_rr}

    def bank(kind):
        lst = bank_rr[kind]
        b = lst[bank_ctr[kind] % len(lst)]
        bank_ctr[kind] += 1
        return b

    def PK(b):
        return ("ps", b)

    E = {"pe": nc.tensor, "act": nc.scalar, "dve": nc.vector, "pool": nc.gpsimd, "sp": nc.sync}

    ew_rr = [0]

    def act_op(out, in_, func, reads, writes, scale=1.0, bias=None, accum_out=None):
        def fn():
            kw = {}
            if bias is not None:
                kw["bias"] = bias
            if accum_out is not None:
                kw["accum_out"] = accum_out
            return nc.scalar.activation(out=out, in_=in_, func=func, scale=scale, **kw)
        return P.add("act", fn, reads=reads, writes=writes)

    def tt(eng, out, in0, in1, op, reads, writes):
        e = E[eng]
        return P.add(eng, lambda: e.tensor_tensor(out=out, in0=in0, in1=in1, op=op), reads=reads, writes=writes)

    def ts(eng, out, in0, s1, s2, op0, op1, reads, writes):
        e = E[eng]
        if op1 is None:
            return P.add(eng, lambda: e.tensor_scalar(out=out, in0=in0, scalar1=s1, scalar2=None, op0=op0),
                         reads=reads, writes=writes)
        return P.add(eng, lambda: e.tensor_scalar(out=out, in0=in0, scalar1=s1, scalar2=s2, op0=op0, op1=op1),
                     reads=reads, writes=writes)

    def stt(eng, out, in0, scalar, in1, op0, op1, reads, writes):
        e = E[eng]
        return P.add(eng, lambda: e.scalar_tensor_tensor(out=out, in0=in0, scalar=scalar, in1=in1, op0=op0, op1=op1),
                     reads=reads, writes=writes)

    def cp(eng, out, in_, reads, writes):
        if eng == "act":
            return P.add("act", lambda: nc.scalar.copy(out=out, in_=in_), reads=reads, writes=writes)
        e = E[eng]
        return P.add(eng, lambda: e.tensor_copy(out=out, in_=in_), reads=reads, writes=writes)

    def memset(eng, ap, val, writes):
        e = E[eng]
        return P.add(eng, lambda: e.memset(ap, val), writes=writes)

    def dma(eng, out, in_, reads, writes, key, slow=False):
        e = E[eng]
        if slow:
            return P.add(eng, lambda: e.dma_start(out=out, in_=in_, allow_slow_non_contiguous=True),
                         reads=reads, writes=writes, dma=key)
        return P.add(eng, lambda: e.dma_start(out=out, in_=in_), reads=reads, writes=writes, dma=key)

    def transpose(out, in_, reads, writes):
        return P.add("pe", lambda: nc.tensor.transpose(out=out, in_=in_, identity=ident[:, :]),
                     reads=list(reads) + ["ident"], writes=writes)

    P.small = True
    memset("pool", ident[:, :], 1.0, ["ident"])
    P.add("pool", lambda: nc.gpsimd.affine_select(out=ident[:, :], in_=ident[:, :], pattern=[[1, 128]],
                                                  compare_op=ALU.is_ge, fill=0.0, base=0, channel_multiplier=-1),
          writes=["ident"])
    P.add("pool", lambda: nc.gpsimd.affine_select(out=ident[:, :], in_=ident[:, :], pattern=[[-1, 128]],
                                                  compare_op=ALU.is_ge, fill=0.0, base=0, channel_multiplier=1),
          writes=["ident"])
    memset("pool", ones_b[:, :], 1.0, ["ones_b"])
    memset("pool", rows_b[:, 0:128], 1.0, ["rows_b"])
    memset("pool", ccol[:, 0:1], EPS, ["ccol"])
    memset("pool", ccol[:, 1:2], float(np.pi * SHR), ["ccol"])
    memset("pool", ccol[:, 2:3], float(-np.pi), ["ccol"])
    memset("pool", ccol[:, 3:4], float(-np.pi / 2), ["ccol"])
    memset("pool", ccol[:, 4:5], 0.0, ["ccol"])
    memset("pool", ccol[:, 5:6], float(np.pi / 2 * SHR), ["ccol"])
    memset("pool", maskcol[:, :], 1.0, ["maskcol"])
    P.add("pool", lambda: nc.gpsimd.affine_select(out=maskcol[:, :], in_=maskcol[:, :], pattern=[[-16, 8]],
                                                  compare_op=ALU.is_ge, fill=0.0, base=0, channel_multiplier=1),
          writes=["maskcol"])
    P.add("pool", lambda: nc.gpsimd.affine_select(out=maskcol[:, :], in_=maskcol[:, :], pattern=[[16, 8]],
                                                  compare_op=ALU.is_ge, fill=0.0, base=15, channel_multiplier=-1),
          writes=["maskcol"])
    memset("pool", cc[:, :, :, :], 0.0, ["cc"])
    memset("pool", s5c[:, :, :], 0.0, [("s5c", l_, g_) for l_ in range(L) for g_ in range(NG)])

    P.small = False
    def colload(dst, src_vec, key):
        dma("pool", dst, src_vec.rearrange("(k p) -> p k", p=128), [], [key], ("cst", 0), slow=True)

    for l in range(L):
        colload(gvec[:, l, 0:16], mix_norm_d[l], "gvec")
        colload(gvec[:, l, 16:32], ffn_norm_d[l], "gvec")
        colload(gvec[:, l, 32:48], ple_norm_d[l], "gvec")
        colload(gvec[:, l, 48:56], out_norm_a_d[l], "gvec")
        colload(gvec[:, l, 56:64], out_norm_b_d[l], "gvec")
        colload(gvec[:, l, 64:72], s5_d_d[l], "gvec")
        colload(gvec[:, l, 72:80], glu_b_d[l], "gvec")
        for a in range(3):
            colload(convc[:, l, a, :], conv_w_d[l, a], "convc")
        colload(convc[:, l, 3, :], conv_b_d[l], "convc")
    colload(fin_g[:, :], final_norm_d, "fin_g")

    pr_f = [pcarve(4096, F32) for _ in range(2)]
    for l in range(L):
        dma("pool", pr_f[0][:, :], sgu_norm_d[l].partition_broadcast(128), [], ["prf0"], ("cst", 0))
        cp("dve", sgu_g[:, l, :], pr_f[0][:, :], ["prf0"], ["sgu_g"])
    for l in range(L):
        sb_row = pr_f[0]
        dma("pool", sb_row[0:1, :], sgu_b_d[l:l + 1].rearrange("l h t -> l (h t)"), [], ["prf0"], ("cst", 0))
        cp("dve", rows_b[0:1, 128 + l * 1024:128 + (l + 1) * 1024], sb_row[0:1, :], ["prf0"], ["rows_b"])
        for h in range(8):
            wst = pr_f[1]
            dma("pool", wst[:, 0:128], sgu_w_d[l, h], [], ["prf1"], ("cst", 0))
            b = bank("aux")
            transpose(ps[b][:, 0:128], wst[:, 0:128], ["prf1"], [PK(b)])
            cp("dve", wst[:, 128:256], ps[b][:, 0:128], [PK(b)], ["prf1b"])
            P.add("pool", lambda wst=wst: nc.gpsimd.affine_select(
                out=wst[:, 128:256], in_=wst[:, 128:256], pattern=[[1, 128]], compare_op=ALU.is_ge, fill=0.0,
                base=0, channel_multiplier=-1), reads=["prf1b"], writes=["prf1b"])
            cp("dve", wmT[:, l, h, :], wst[:, 128:256], ["prf1b"], ["wmT"])

    def pc64():
        return pcarve(256, F32)

    iota_t = pcarve(SEQ * 4, F32)
    P.add("pool", lambda: nc.gpsimd.iota(iota_t[:, :], pattern=[[1, SEQ]], base=0, channel_multiplier=0,
                                         allow_small_or_imprecise_dtypes=True), writes=["iota"])
    LR, LI, DT, TH, T1, T2, T3, T4, ABR, ABI, QR, QI, QA, QB, QA2, QB2 = [pc64() for _ in range(16)]
    T1i = pcarve(256, I32)
    BR2 = pcarve(4096, F32, "p (g c) -> p g c", c=16)
    BI2 = pcarve(4096, F32, "p (g c) -> p g c", c=16)
    W1 = pcarve(4096, F32, "p (g c) -> p g c", c=16)
    W2 = pcarve(4096, F32, "p (g c) -> p g c", c=16)
    WT = pcarve(4096, F32, "p (g c) -> p g c", c=16)
    CC1 = pcarve(512, F32)
    CC2 = pcarve(512, F32)
    CCT = pcarve(512, F32)
    pbst = [pcarve(4096, BF16, "p (a n) -> p a n", a=16) for _ in range(2)]
    pcst = [pcarve(4096, BF16, "p (a n) -> p a n", a=16) for _ in range(2)]
    phs = [pcarve(SEQ * 4, F32) for _ in range(2)]
    phk = pcarve(SEQ * 4, I32)
    phf = pcarve(SEQ * 4, F32)
    tabst = [pcarve(SEQ * 2 * 2, BF16, "p (a n) -> p a n", a=2) for _ in range(2)]
    s5_end = pcur[0]

    def range_reduce(eng, x, kint, kf, kkey, kfkey, key):
        ts(eng, kint, x, float(1.0 / TWO_PI), None, ALU.mult, None, [key], [kkey])
        cp(eng, kf, kint, [kkey], [kfkey])
        stt(eng, x, kf, -TWO_PI, x, ALU.mult, ALU.add, [kfkey, key], [key])
        ts(eng, kf, x, 0.0, TWO_PI, ALU.is_lt, ALU.mult, [key], [kfkey])
        tt(eng, x, x, kf, ALU.add, [key, kfkey], [key])

    def sin_of(out, r, reads, writes):
        return act_op(out, r, AF.Sin, reads, writes, scale=-SHR, bias=ccol[:, 1:2])

    def cos_of(out, r, tmp, reads, writes, tkey):
        act_op(tmp, r, AF.Abs, reads, [tkey], scale=1.0, bias=ccol[:, 2:3])
        return act_op(out, tmp, AF.Sin, [tkey], writes, scale=SHR, bias=ccol[:, 3:4])

    def s5_prologue():
      for l in range(L):
          P.small = True
          for half in range(2):
              hs = slice(half * 64, half * 64 + 64)
              dma("pool", LR[hs, :], lam_re_d[l].rearrange("g n -> n g"), [], ["LR"], ("cst", 0), slow=True)
              dma("pool", LI[hs, :], lam_im_d[l].rearrange("g n -> n g"), [], ["LI"], ("cst", 0), slow=True)
              dma("pool", BR2[hs, :, :], b_re_d[l].rearrange("g n c -> n g c"), [], ["BR2"], ("cst", 0))
              dma("pool", BI2[hs, :, :], b_im_d[l].rearrange("g n c -> n g c"), [], ["BI2"], ("cst", 0))
          dma("pool", DT[:, :], log_dt_d[l].partition_broadcast(128), [], ["DT"], ("cst", 0))
          act_op(DT[:, :], DT[:, :], AF.Exp, ["DT"], ["DT"])
          tt("dve", T1[:, :], LR[:, :], DT[:, :], ALU.mult, ["LR", "DT"], ["T1"])
          act_op(amag[:, l, :], T1[:, :], AF.Exp, ["T1"], ["amag"])
          tt("dve", TH[:, :], LI[:, :], DT[:, :], ALU.mult, ["LI", "DT"], ["TH"])
          cp("dve", T2[:, :], TH[:, :], ["TH"], ["T2"])
          range_reduce("dve", T2[:, :], T1i[:, :], T3[:, :], "T1i", "T3", "T2")
          sin_of(ABI[:, :], T2[:, :], ["T2", "ccol"], ["ABI"])
          cos_of(ABR[:, :], T2[:, :], T4[:, :], ["T2", "ccol"], ["ABR"], "T4")
          tt("dve", ABR[:, :], ABR[:, :], amag[:, l, :], ALU.mult, ["ABR", "amag"], ["ABR"])
          tt("dve", ABI[:, :], ABI[:, :], amag[:, l, :], ALU.mult, ["ABI", "amag"], ["ABI"])
          tt("dve", T1[:, :], LR[:, :], LR[:, :], ALU.mult, ["LR"], ["T1"])
          tt("dve", T3[:, :], LI[:, :], LI[:, :], ALU.mult, ["LI"], ["T3"])
          tt("dve", T1[:, :], T1[:, :], T3[:, :], ALU.add, ["T1", "T3"], ["T1"])
          P.add("dve", lambda: nc.vector.reciprocal(out=T1[:, :], in_=T1[:, :]), reads=["T1"], writes=["T1"])
          ts("dve", T3[:, :], ABR[:, :], -1.0, None, ALU.add, None, ["ABR"], ["T3"])
          tt("dve", QR[:, :], T3[:, :], LR[:, :], ALU.mult, ["T3", "LR"], ["QR"])
          tt("dve", T4[:, :], ABI[:, :], LI[:, :], ALU.mult, ["ABI", "LI", "T4"], ["T4"])
          tt("dve", QR[:, :], QR[:, :], T4[:, :], ALU.add, ["QR", "T4"], ["QR"])
          tt("dve", QR[:, :], QR[:, :], T1[:, :], ALU.mult, ["QR", "T1"], ["QR"])
          tt("dve", QI[:, :], ABI[:, :], LR[:, :], ALU.mult, ["ABI", "LR"], ["QI"])
          tt("dve", T4[:, :], T3[:, :], LI[:, :], ALU.mult, ["T3", "LI"], ["T4"])
          tt("dve", QI[:, :], QI[:, :], T4[:, :], ALU.subtract, ["QI", "T4"], ["QI"])
          tt("dve", QI[:, :], QI[:, :], T1[:, :], ALU.mult, ["QI", "T1"], ["QI"])
          lo, hi = slice(0, 64), slice(64, 128)
          cp("dve", QA[lo, :], QR[lo, :], ["QR"], ["QA"])
          cp("dve", QA[hi, :], QI[hi, :], ["QI"], ["QA"])
          ts("dve", QB[lo, :], QI[lo, :], -1.0, None, ALU.mult, None, ["QI"], ["QB"])
          cp("dve", QB[hi, :], QR[hi, :], ["QR"], ["QB"])
          cp("dve", QA2[lo, :], QI[lo, :], ["QI"], ["QA2"])
          ts("dve", QA2[hi, :], QR[hi, :], -1.0, None, ALU.mult, None, ["QR"], ["QA2"])
          cp("dve", QB2[lo, :], QR[lo, :], ["QR"], ["QB2"])
          cp("dve", QB2[hi, :], QI[hi, :], ["QI"], ["QB2"])

          P.small = False

          def bc(q):
              return q[:, :].unsqueeze(2).broadcast_to([128, NG, 16])
          tt("dve", W1[:, :, :], BR2[:, :, :], bc(QA), ALU.mult, ["BR2", "QA"], ["W1"])
          tt("dve", WT[:, :, :], BI2[:, :, :], bc(QB), ALU.mult, ["BI2", "QB"], ["WT"])
          tt("dve", W1[:, :, :], W1[:, :, :], WT[:, :, :], ALU.add, ["W1", "WT"], ["W1"])
          tt("dve", W2[:, :, :], BR2[:, :, :], bc(QA2), ALU.mult, ["BR2", "QA2"], ["W2"])
          tt("dve", WT[:, :, :], BI2[:, :, :], bc(QB2), ALU.mult, ["BI2", "QB2"], ["WT"])
          tt("dve", W2[:, :, :], W2[:, :, :], WT[:, :, :], ALU.add, ["W2", "WT"], ["W2"])
          for mt in range(8):
              st = pbst[mt % 2]
              skey = f"pbst{mt % 2}"
              for wi, Wx in enumerate((W1, W2)):
                  b = bank("aux")
                  transpose(ps[b][:, 0:128], Wx[:, mt * 8:(mt + 1) * 8, :].rearrange("p g c -> p (g c)"),
                            ["W1", "W2"], [PK(b)])
                  for j in range(8):
                      ts("dve" if j % 2 == 0 else "act" if False else "dve", st[:, 2 * j + wi, :], ps[b][:, 0:128],
                         maskcol[:, j:j + 1], None, ALU.mult, None, [PK(b), "maskcol"], [skey])
              dma("act", s5w_scr[l, mt * 8:(mt + 1) * 8, :, 0:256].rearrange("g p (w n) -> p g w n", w=2),
                  st[:, :, :].rearrange("p (g w) n -> p g w n", w=2), [skey], [], ("pbst", mt % 2))
              g0 = mt * 8
              dma("pool", CC1[:, 0:64], c_re_d[l, g0:g0 + 8].rearrange("g c n -> (g c) n"), [], ["CC1"], ("cst", 0))
              dma("pool", CC2[:, 64:128], c_re_d[l, g0:g0 + 8].rearrange("g c n -> (g c) n"), [], ["CC2"], ("cst", 0))
              dma("pool", CC1[:, 64:128], c_im_d[l, g0:g0 + 8].rearrange("g c n -> (g c) n"), [], ["CC1"], ("cst", 0))
              dma("pool", CC2[:, 0:64], c_im_d[l, g0:g0 + 8].rearrange("g c n -> (g c) n"), [], ["CC2"], ("cst", 0))
              ts("dve", CC1[:, 64:128], CC1[:, 64:128], -1.0, None, ALU.mult, None, ["CC1"], ["CC1"])
              ts("dve", CC2[:, :], CC2[:, :], -1.0, None, ALU.mult, None, ["CC2"], ["CC2"])
              cs = pcst[mt % 2]
              ckey = f"pcst{mt % 2}"
              memset("pool", cs[:, :, :], 0.0, [ckey])
              for wi, Cx in enumerate((CC1, CC2)):
                  b = bank("aux")
                  transpose(ps[b][:, 0:128], Cx[:, :], ["CC1", "CC2"], [PK(b)])
                  for j in range(8):
                      cp("dve", cs[:, 2 * j + wi, 16 * j:16 * j + 16], ps[b][:, 16 * j:16 * j + 16], [PK(b)], [ckey])
              dma("act", s5w_scr[l, mt * 8:(mt + 1) * 8, :, 256:512].rearrange("g p (w n) -> p g w n", w=2),
                  cs[:, :, :].rearrange("p (g w) n -> p g w n", w=2), [ckey], [], ("pcst", mt % 2))
              yield
          for g in range(NG):
              i = g % 2
              ph = phs[i]; pkey = f"ph{i}"
              tb = tabst[i]; tkey = f"tabst{i}"
              P.add("dve", lambda g=g: nc.vector.tensor_scalar(out=phk[:, :], in0=iota_t[:, :], scalar1=TH[:, g:g + 1],
                                                             scalar2=float(1.0 / TWO_PI), op0=ALU.mult, op1=ALU.mult),
                    reads=["iota", "TH"], writes=["phk"])
              ts("dve", phf[:, :], phk[:, :], -TWO_PI, None, ALU.mult, None, ["phk"], ["phf"])
              stt("dve", ph[:, :], iota_t[:, :], TH[:, g:g + 1], phf[:, :], ALU.mult, ALU.add, ["iota", "TH", "phf"], [pkey])
              act_op(tb[:, 1, :], ph[:, :], AF.Sin, [pkey], [tkey], scale=SHR)
              act_op(ph[:, :], ph[:, :], AF.Abs, [pkey], [pkey])
              act_op(tb[:, 0, :], ph[:, :], AF.Sin, [pkey, "ccol"], [tkey], scale=-SHR, bias=ccol[:, 5:6])
              dma("act", tab_scr[l, g], tb[:, :, :], [tkey], [], ("tabst", i))
              yield

    pcur[0] = s5_end
    CW = 1408
    NST = 2
    stin = [pcarve(4 * CW * 4, F32, "p (a c) -> p a c", a=4) for _ in range(NST)]
    stout = [pcarve(4 * CW * 2, BF16, "p (m a c) -> p m a c", a=4, c=128) for _ in range(NST)]
    assert pcur[0] <= ARENA_BYTES, pcur[0]
    cast_engs = ["act", "pool"]

    def conv_gen():
        cvt = 0
        pending = None
        for l in range(L):
            for name, K, M in _weight_specs():
                src = wsrc[name][l]
                KT = K // 128
                dst = wscr[(name, l)].rearrange("m p (k c) -> m p k c", c=128)
                for c0 in range(0, M, CW):
                    cw = min(CW, M - c0)
                    mw = cw // 128
                    for kq in range(0, KT, 4):
                        na = min(4, KT - kq)
                        i = cvt % NST
                        cvt += 1
                        si, so = stin[i], stout[i]
                        dma("sp", si[:, 0:na, 0:cw],
                            src[kq * 128:(kq + na) * 128, c0:c0 + cw].rearrange("(a p) c -> p a c", p=128),
                            [], [f"stin{i}"], ("stin", i))

                        def finish(i=i, si=si, so=so, na=na, cw=cw, mw=mw, dst=dst, c0=c0, kq=kq, cvt=cvt):
                            for a in range(na):
                                eng = cast_engs[(cvt + a) % 2]
                                cp(eng, so[:, 0:mw, a, :], si[:, a, 0:cw].rearrange("p (m c) -> p m c", c=128),
                                   [f"stin{i}"], [f"stout{i}"])
                            dma("sp", dst[c0 // 128:c0 // 128 + mw, :, kq:kq + na, :].rearrange("m p a c -> p m a c"),
                                so[:, 0:mw, 0:na, :], [f"stout{i}"], [], ("stout", i))
                        if pending is not None:
                            pending()
                        pending = finish
                        yield
        pending()
        yield

    g5 = s5_prologue()
    gc = conv_gen()
    live5, livec = True, True
    while live5 or livec:
        if live5:
            try:
                next(g5)
            except StopIteration:
                live5 = False
        if livec:
            try:
                next(gc)
            except StopIteration:
                livec = False

    PRO_KEYS = (["prf0", "prf1", "prf1b", "iota", "LR", "LI", "DT", "TH", "T1", "T2", "T3", "T4", "ABR", "ABI",
                 "QR", "QI", "QA", "QB", "QA2", "QB2", "BR2", "BI2", "W1", "W2", "WT", "CC1", "CC2",
                 "pbst0", "pbst1", "pcst0", "pcst1", "ph0", "ph1", "phk", "phf",
                 "T1i", "tabst0", "tabst1", "stin0", "stin1", "stin2", "stout0", "stout1", "stout2"])
    XALL = [("xT", k_) for k_ in range(KT_D)]
    MAIN_KEYS = (XALL + ["hT", "mixT", "tmp0", "tmp1", "tmp2", "tmp3", "sqb0", "sqb1", "rstd_x", "rstd_a",
                  "rstd_b", "smallc", "hid", "ab0", "ab1", "ab2", "ab3", "ptok", "pT", "io0", "io1"]
                 + UNION_KEYS_MIX + [f"w{i}" for i in range(NWB)] + [f"gw{i}" for i in range(NTB)] + [f"gt{i}" for i in range(NTB)])
    for eng in ("sp", "pool", "act", "dve", "pe"):
        if eng in ("sp",):
            continue
    P.add("dve", lambda: nc.vector.memset(smallc[:, :], 0.0), reads=[], writes=PRO_KEYS + MAIN_KEYS + ["DRAMSCR"])

    wring = [0]

    def wload(src_ap, nbytes_cols):
        i = wring[0] % NWB
        wring[0] += 1
        buf = wbuf[i]
        dma("sp", buf[:, 0:nbytes_cols], src_ap, ["DRAMSCR"], [f"w{i}"], ("w", i))
        return buf, f"w{i}"

    tring = [0]

    def gload(l, g, t0):
        i = tring[0] % NTB
        tring[0] += 1
        dma("sp", gring[i][:, 0:512], s5w_scr[l, g], ["DRAMSCR"], [f"gw{i}"], ("gw", i))
        dma("sp", gring[i][:, 512:1536].rearrange("p (a n) -> p a n", a=2), tab_scr[l, g, :, :, t0:t0 + NT],
            ["DRAMSCR"], [f"gt{i}"], ("gt", i))
        return gring[i], f"gw{i}", f"gt{i}"

    tmp_ctr = [0]

    def tmp():
        i = tmp_ctr[0] % 4
        tmp_ctr[0] += 1
        return tmpf[i], f"tmp{i}"

    sq_ctr = [0]

    def sqt():
        i = sq_ctr[0] % 2
        sq_ctr[0] += 1
        return sqb[i], f"sqb{i}"

    def stat_piece(b, k, nk=KT_D):
        sq_, skey = sqt()
        act_op(sq_[:, :], xT[:, k, :], AF.Square, [("xT", k)], [skey])
        P.add("pe", lambda sq_=sq_, k=k, b=b: nc.tensor.matmul(ps[b][:, :], lhsT=ones_b[:, :], rhs=sq_[:, :],
                                                                 start=(k == 0), stop=(k == nk - 1)),
              reads=[skey, "ones_b"], writes=[PK(b)])

    def rms_stats_x():
        b = bank("stat")
        for k in range(KT_D):
            stat_piece(b, k)
        return b

    def finish_rstd(b, rstd, rkey, dimn):
        act_op(rstd[:, :], ps[b][:, :], AF.Sqrt, [PK(b), "ccol"], [rkey], scale=1.0 / dimn, bias=ccol[:, 0:1])
        P.add("dve", lambda: nc.vector.reciprocal(out=rstd[:, :], in_=rstd[:, :]), reads=[rkey], writes=[rkey])

    def norm_to_hT(l, goff, b=None):
        if b is None:
            b = rms_stats_x()
        finish_rstd(b, rstd_x, "rstd_x", float(D))
        for k in range(KT_D):
            stt("dve", hT[:, k, :], xT[:, k, :], gvec[:, l, goff + k:goff + k + 1], rstd_x[:, :], ALU.mult, ALU.mult,
                [("xT", k), "gvec", "rstd_x"], ["hT"])

    def mm_group(b, wv, wkey, kt, rhs_fn, rkeys, cols=slice(0, NT)):
        def fn():
            ins = None
            for k in range(kt):
                ins = nc.tensor.matmul(ps[b][:, cols], lhsT=wv[:, k * 128:(k + 1) * 128], rhs=rhs_fn(k),
                                       start=(k == 0), stop=(k == kt - 1))
            return ins
        return P.add("pe", fn, reads=[wkey] + list(rkeys), writes=[PK(b)])

    def unit(name, l, m):
        return wscr[(name, l)][m]

    tiles = [(s, i) for s in range(n_seq) for i in range(n_tiles_per_seq)]
    first_mixer_write = [True]
    for (s, ti) in tiles:
        t0 = ti * NT
        for c in range(NCH):
            st = iostage[c % 2]; skey = f"io{c % 2}"
            extra_w = []
            dma("pool", st[:, :], x_d[s, t0 + c * 128:t0 + (c + 1) * 128, :], [], [skey] + UNION_KEYS_MIX + UNION_KEYS_FFN,
                ("io", c % 2))
            for kq in range(4):
                b = bank("mm")
                def fn(b=b, st=st, kq=kq):
                    ins = None
                    for kk in range(4):
                        k = kq * 4 + kk
                        ins = nc.tensor.transpose(out=ps[b][:, kk * 128:(kk + 1) * 128],
                                                  in_=st[:, k * 128:(k + 1) * 128], identity=ident[:, :])
                    return ins
                P.add("pe", fn, reads=[skey, "ident"], writes=[PK(b)])
                cp("act" if kq % 2 else "dve", xT[:, kq * 4:(kq + 1) * 4, c * 128:(c + 1) * 128],
                   ps[b].rearrange("p (k n) -> p k n", k=4), [PK(b)], [("xT", kq * 4 + kk_) for kk_ in range(4)])
        if ti == 0:
            for l in range(L):
                memset("pool", cc[:, l, :, :], 0.0, ["cc"])
                memset("pool", s5c[:, l, :], 0.0, [("s5c", l, g_) for g_ in range(NG)])

        sb_mix = None
        for l in range(L):
            norm_to_hT(l, 0, sb_mix)
            sb_mix = None
            def xs_unit(m):
                wv, wkey = wload(unit("w_in", l, 16 + m), KT_D * 128)
                b = bank("mm2")
                mm_group(b, wv, wkey, KT_D, lambda k: hT[:, k, :], ["hT"])
                cp("act", xsb[:, m, :], ps[b][:, :], [PK(b)], ["xsb"] + mixw(l, s, ti, "xsb"))

            def u_unit(m):
                wv, wkey = wload(unit("w_in", l, m), KT_D * 128)
                b = bank("mm2")
                mm_group(b, wv, wkey, KT_D, lambda k: hT[:, k, :], ["hT"])
                act_op(uT[:, m, :], ps[b][:, :], AF.Gelu_apprx_tanh, [PK(b)], ["uT"] + mixw(l, s, ti, "uT"))

            def v_unit(m):
                wv, wkey = wload(unit("w_in", l, 8 + m), KT_D * 128)
                b = bank("mm2")
                def fn(b=b, wv=wv):
                    ins = None
                    for c in range(NCH):
                        for k in range(KT_D):
                            ins = nc.tensor.matmul(ps[b][:, c * 128:(c + 1) * 128],
                                                   lhsT=hT[:, k, c * 128:(c + 1) * 128],
                                                   rhs=wv[:, k * 128:(k + 1) * 128],
                                                   start=(k == 0), stop=(k == KT_D - 1))
                    return ins
                P.add("pe", fn, reads=[wkey, "hT"], writes=[PK(b)])
                act_op(vtok[:, :, m * 128:(m + 1) * 128], ps[b].rearrange("p (c d) -> p c d", c=NCH),
                       AF.Gelu_apprx_tanh, [PK(b)], ["vtok"] + mixw(l, s, ti, "vtok"))

            def ln_unit(c):
                j, jkey = tmp()
                j2, j2key = tmp()
                sc = smallc
                act_op(j[:, :].bitcast(BF16), vtok[:, c, :], AF.Identity, ["vtok"], [jkey, "smallc"],
                       accum_out=sc[:, 0:1])
                act_op(j[:, :].bitcast(BF16), vtok[:, c, :], AF.Square, ["vtok"], [jkey, "smallc"],
                       accum_out=sc[:, 1:2])
                cp("act", sc[:, 8:9], sc[:, 1:2], ["smallc"], ["smallc"])
                P.small = True
                ts("dve", sc[:, 2:3], sc[:, 0:1], 1.0 / D_SGU, None, ALU.mult, None, ["smallc"], ["smallc"])
                tt("dve", sc[:, 3:4], sc[:, 2:3], sc[:, 2:3], ALU.mult, ["smallc"], ["smallc"])
                stt("dve", sc[:, 4:5], sc[:, 1:2], 1.0 / D_SGU, sc[:, 3:4], ALU.mult, ALU.subtract,
                    ["smallc"], ["smallc"])
                act_op(sc[:, 5:6], sc[:, 4:5], AF.Sqrt, ["smallc", "ccol"], ["smallc"], scale=1.0, bias=ccol[:, 0:1])
                P.add("dve", lambda sc=sc: nc.vector.reciprocal(out=sc[:, 6:7], in_=sc[:, 5:6]),
                      reads=["smallc"], writes=["smallc"])
                stt("dve", sc[:, 7:8], sc[:, 2:3], -1.0, sc[:, 6:7], ALU.mult, ALU.mult, ["smallc"], ["smallc"])
                P.small = False
                for hh in range(2):
                    cs_ = slice(hh * 512, (hh + 1) * 512)
                    act_op(j2[:, :], vtok[:, c, cs_], AF.Identity, ["vtok", "smallc"], [j2key],
                           scale=sc[:, 6:7], bias=sc[:, 7:8])
                    tt("dve", vtok[:, c, cs_], j2[:, :], sgu_g[:, l, cs_], ALU.mult, [j2key, "sgu_g"], ["vtok"])

            bss_box = [None]

            def sgu_unit(h):
                if h == 0:
                    bss_box[0] = bank("stat")
                bss = bss_box[0]
                b = bank("mm2")
                def fn(b=b, h=h, l=l):
                    ins = None
                    for c in range(NCH):
                        nc.tensor.matmul(ps[b][:, c * 128:(c + 1) * 128], lhsT=vtok[:, c, h * 128:(h + 1) * 128],
                                         rhs=wmT[:, l, h, :], start=True, stop=False)
                        ins = nc.tensor.matmul(ps[b][:, c * 128:(c + 1) * 128], lhsT=rows_b[0:1, 0:128],
                                               rhs=rows_b[0:1, 128 + l * 1024 + h * 128:128 + l * 1024 + (h + 1) * 128],
                                               start=False, stop=True)
                    return ins
                P.add("pe", fn, reads=["vtok", "wmT", "rows_b"], writes=[PK(b)])
                ya, yakey = tmp()
                tt("dve", ya[:, :], ps[b][:, :], uT[:, h, :], ALU.mult, [PK(b), "uT"], [yakey])
                act_op(mixT[:, h, :], ya[:, :], AF.Identity, [yakey, "gvec"], ["mixT"], scale=gvec[:, l, 48 + h:49 + h])
                sq, sqkey = sqt()
                act_op(sq[:, :], ya[:, :], AF.Square, [yakey], [sqkey])
                P.add("pe", lambda sq=sq, h=h, bss=bss: nc.tensor.matmul(ps[bss][:, :], lhsT=ones_b[:, :], rhs=sq[:, :],
                                                                         start=(h == 0), stop=(h == 7)),
                      reads=[sqkey, "ones_b"], writes=[PK(bss)])
                if h == 7:
                    finish_rstd(bss, rstd_a, "rstd_a", float(D_SGU))

            def s5_A(g):
                mt, j = divmod(g, 8)
                gr, gwkey, gtkey = gload(l, g, t0)
                b1 = bank("s5p"); b2 = bank("s5p")
                P.add("pe", lambda b1=b1, gr=gr, mt=mt: nc.tensor.matmul(
                    ps[b1][:, :], lhsT=gr[:, 0:128], rhs=xsb[:, mt, :], start=True, stop=True),
                    reads=[gwkey, "xsb"], writes=[PK(b1)])
                P.add("pe", lambda b2=b2, gr=gr, mt=mt: nc.tensor.matmul(
                    ps[b2][:, :], lhsT=gr[:, 128:256], rhs=xsb[:, mt, :], start=True, stop=True),
                    reads=[gwkey, "xsb"], writes=[PK(b2)])
                assert b2 == b1 + 1
                pb = s5pb[g % 2]; pbkey = f"s5pb{g % 2}"
                cp("act", pb[:, :], psall[:, b1 * 512:(b1 + 2) * 512], [PK(b1), PK(b2)], [pbkey] + mixw(l, s, ti, pbkey))
                return (g, mt, j, gr, gwkey, gtkey, b1, b2)

            def s5_BD(a, by):
                g, mt, j, gr, gwkey, gtkey, b1, b2 = a
                assert b2 == b1 + 1
                pb = s5pb[g % 2]; pbkey = f"s5pb{g % 2}"
                t12 = s5t12[g % 2]; t12key = f"s5t12{g % 2}"
                G = s5g[g % NS5]; gkey = f"s5g{g % NS5}"
                G12 = s5g12[g % NS5]; g12key = f"s5g12{g % NS5}"
                tab = gr[:, 512:1536]
                tt("dve", t12[:, :], pb[:, :], tab, ALU.mult, [pbkey, gtkey], [t12key] + mixw(l, s, ti, t12key))
                tt("dve", G[:, :], t12[:, 0:512], t12[:, 512:1024], ALU.add, [t12key], [gkey] + mixw(l, s, ti, gkey))
                P.add("dve", lambda G=G, l=l, g=g: nc.vector.tensor_tensor_scan(
                    out=G[:, :], data0=amag[:, l, g:g + 1].broadcast_to([128, NT]), data1=G[:, :],
                    initial=s5c[:, l, g:g + 1], op0=ALU.mult, op1=ALU.add),
                    reads=[gkey, "amag", ("s5c", l, g)], writes=[gkey])
                cp("act", s5c[:, l, g:g + 1], G[:, NT - 1:NT], [gkey], [("s5c", l, g)])
                tt("dve", G12[:, :, :], G[:, :].unsqueeze(1).broadcast_to([128, 2, NT]),
                   tab.rearrange("p (a n) -> p a n", a=2), ALU.mult, [gkey, gtkey], [g12key] + mixw(l, s, ti, g12key))
                return (g, mt, j, gr, gwkey, G12, g12key, by)

            def s5_F(f):
                g, mt, j, gr, gwkey, G12, g12key, by = f
                def fn(by=by, gr=gr, j=j, G12=G12):
                    nc.tensor.matmul(ps[by][:, :], lhsT=gr[:, 256:384], rhs=G12[:, 0, :],
                                     start=(j == 0), stop=False)
                    return nc.tensor.matmul(ps[by][:, :], lhsT=gr[:, 384:512], rhs=G12[:, 1, :],
                                            start=False, stop=(j == 7))
                P.add("pe", fn, reads=[gwkey, g12key], writes=[PK(by)])
                if j == 7:
                    yt, ytkey = tmp()
                    stt("dve", yt[:, :], xsb[:, mt, :], gvec[:, l, 64 + mt:65 + mt], ps[by][:, :], ALU.mult, ALU.add,
                        ["xsb", "gvec", PK(by)], [ytkey])
                    act_op(zf[:, mt, :], yt[:, :], AF.Gelu_apprx_tanh, [ytkey], ["zf"] + mixw(l, s, ti, "zf"))
                    cp("act", zb[:, mt, :], zf[:, mt, :], ["zf"], ["zb"] + mixw(l, s, ti, "zb"))

            def s5_pipeline():
                by = bank("acc")
                a_next = s5_A(0)
                f_prev = None
                for g in range(NG):
                    a_cur = a_next
                    a_next = s5_A(g + 1) if g + 1 < NG else None
                    f_cur = s5_BD(a_cur, by)
                    if f_prev is not None:
                        s5_F(f_prev)
                    f_prev = f_cur
                    yield
                s5_F(f_prev)
                yield

            others = ([lambda m=m: xs_unit(m) for m in range(8)]
                      + [lambda m=m: u_unit(m) for m in range(8)] + [lambda m=m: v_unit(m) for m in range(8)]
                      + [lambda c=c: ln_unit(c) for c in range(NCH)] + [lambda h=h: sgu_unit(h) for h in range(8)])
            gen = s5_pipeline()
            n_steps = NG + 1
            done = 0
            for ui, ufn in enumerate(others):
                ufn()
                target = (n_steps * (ui + 1)) // len(others)
                while done < target:
                    next(gen)
                    done += 1
            for _ in gen:
                pass
            bss = bank("stat")
            for m in range(8):
                wv, wkey = wload(unit("s5_glu_w", l, m), 8 * 128)
                b = bank("mm")
                mm_group(b, wv, wkey, 8, lambda k: zb[:, k, :], ["zb"])
                gt, gtkey = tmp()
                act_op(gt[:, :], ps[b][:, :], AF.Sigmoid, [PK(b), "gvec"], [gtkey], bias=gvec[:, l, 72 + m:73 + m])
                yb, ybkey = tmp()
                tt("dve", yb[:, :], zf[:, m, :], gt[:, :], ALU.mult, ["zf", gtkey], [ybkey])
                act_op(mixT[:, 8 + m, :], yb[:, :], AF.Identity, [ybkey, "gvec"], ["mixT"],
                       scale=gvec[:, l, 56 + m:57 + m])
                sq, sqkey = sqt()
                act_op(sq[:, :], yb[:, :], AF.Square, [ybkey], [sqkey])
                P.add("pe", lambda sq=sq, m=m, bss=bss: nc.tensor.matmul(ps[bss][:, :], lhsT=ones_b[:, :], rhs=sq[:, :],
                                                                         start=(m == 0), stop=(m == 7)),
                      reads=[sqkey, "ones_b"], writes=[PK(bss)])
            finish_rstd(bss, rstd_b, "rstd_b", float(D_SSM))
            sb_next = bank("stat")
            for m in range(KT_D):
                wv, wkey = wload(unit("w_out", l, m), KT_D * 128)
                ba = bank("mm"); bb = bank("mm")
                def fn(ba=ba, bb=bb, wv=wv):
                    ins = None
                    for k in range(8):
                        nc.tensor.matmul(ps[ba][:, :], lhsT=wv[:, k * 128:(k + 1) * 128], rhs=mixT[:, k, :],
                                         start=(k == 0), stop=(k == 7))
                    for k in range(8, 16):
                        ins = nc.tensor.matmul(ps[bb][:, :], lhsT=wv[:, k * 128:(k + 1) * 128], rhs=mixT[:, k, :],
                                               start=(k == 8), stop=(k == 15))
                    return ins
                P.add("pe", fn, reads=[wkey, "mixT"], writes=[PK(ba), PK(bb)])
                ta, takey = tmp()
                tt("dve", ta[:, :], ps[ba][:, :], rstd_a[:, :], ALU.mult, [PK(ba), "rstd_a"], [takey])
                if "nosgu" not in debug:
                    tt("pool", xT[:, m, :], xT[:, m, :], ta[:, :], ALU.add, [("xT", m), takey], [("xT", m)])
                tb2, tb2key = tmp()
                tt("dve", tb2[:, :], ps[bb][:, :], rstd_b[:, :], ALU.mult, [PK(bb), "rstd_b"], [tb2key])
                if "nos5" not in debug:
                    tt("pool", xT[:, m, :], xT[:, m, :], tb2[:, :], ALU.add, [("xT", m), tb2key], [("xT", m)])
                stat_piece(sb_next, m)

            if "dump" in debug and l == 0 and s == 0 and ti == 0:
                dumps = {"mixT": (mixT, BF16, 8192, ["mixT"]), "zf": (zf, F32, 4096, ["zf"]),
                         "vtok": (vtok, BF16, 4096, ["vtok"]), "uT": (uT, BF16, 4096, ["uT"]),
                         "xsb": (xsb, BF16, 4096, ["xsb"]), "rstd_a": (rstd_a, F32, 512, ["rstd_a"]),
                         "rstd_b": (rstd_b, F32, 512, ["rstd_b"]), "zb": (zb, BF16, 4096, ["zb"])}
                for dn, (buf, dt_, n_, keys_) in dumps.items():
                    dd = nc.dram_tensor("dbg_" + dn, [128, n_], dt_, kind="ExternalOutput").ap()
                    src_ = buf if len(buf.shape) == 2 else buf.rearrange("p a b -> p (a b)")
                    dma("pool", dd[:, :], src_, keys_, ["OUT"], ("dbg", dn))
            norm_to_hT(l, 16, sb_next)
            NJ = D_FF // 128
            for j in range(NJ):
                wg, wgkey = wload(unit("ffn_w_up", l, j), KT_D * 128)
                wu, wukey = wload(unit("ffn_w_up", l, NJ + j), KT_D * 128)
                bg = bank("mm"); bu = bank("mm")
                mm_group(bg, wg, wgkey, KT_D, lambda k: hT[:, k, :], ["hT"])
                mm_group(bu, wu, wukey, KT_D, lambda k: hT[:, k, :], ["hT"])
                cres = []
                for half, (bx, jj) in enumerate(((bg, j), (bu, NJ + j))):
                    ab = abuf[(2 * j + half) % 4]; abkey = f"ab{(2 * j + half) % 4}"
                    first = (j < 2)
                    cp("pool", ab[:, 0:2], cc[:, l, jj, :], ["cc"],
                       [abkey] + (UNION_KEYS_MIX if first else []))
                    cp("act", ab[:, 2:2 + NT], ps[bx][:, :], [PK(bx)], [abkey])
                    cp("pool", cc[:, l, jj, :], ab[:, NT:NT + 2], [abkey], ["cc"])
                    ct, ctkey = tmp()
                    act_op(ct[:, :], ps[bx][:, :], AF.Identity, [PK(bx), "convc"], [ctkey],
                           scale=convc[:, l, 2, jj:jj + 1], bias=convc[:, l, 3, jj:jj + 1])
                    stt("dve", ct[:, :], ab[:, 1:1 + NT], convc[:, l, 1, jj:jj + 1], ct[:, :], ALU.mult, ALU.add,
                        [abkey, "convc", ctkey], [ctkey])
                    stt("dve", ct[:, :], ab[:, 0:NT], convc[:, l, 0, jj:jj + 1], ct[:, :], ALU.mult, ALU.add,
                        [abkey, "convc", ctkey], [ctkey])
                    cres.append((ct, ctkey))
                (cg, cgkey), (cu, cukey) = cres
                act_op(cg[:, :], cg[:, :], AF.Silu, [cgkey], [cgkey])
                tt("dve", hid[:, j, :], cg[:, :], cu[:, :], ALU.mult, [cgkey, cukey],
                   ["hid"] + (UNION_KEYS_MIX if j == 0 else []))
            sb_next = bank("stat")
            for m in range(KT_D):
                w0, w0key = wload(unit("ffn_w_down", l, m)[:, 0:22 * 128], 22 * 128)
                w1, w1key = wload(unit("ffn_w_down", l, m)[:, 22 * 128:44 * 128], 22 * 128)
                b = bank("mm")
                def fn(b=b, w0=w0, w1=w1):
                    ins = None
                    for k in range(44):
                        wv = w0 if k < 22 else w1
                        kk = k % 22
                        ins = nc.tensor.matmul(ps[b][:, :], lhsT=wv[:, kk * 128:(kk + 1) * 128], rhs=hid[:, k, :],
                                               start=(k == 0), stop=(k == 43))
                    return ins
                P.add("pe", fn, reads=[w0key, w1key, "hid"], writes=[PK(b)])
                if "noffn" not in debug:
                    tt("dve", xT[:, m, :], xT[:, m, :], ps[b][:, :], ALU.add, [("xT", m), PK(b)], [("xT", m)])
                else:
                    cp("dve", tmpf[0][:, :], ps[b][:, :], [PK(b)], ["tmp0"])
                stat_piece(sb_next, m)

            norm_to_hT(l, 32, sb_next)
            for c in range(NCH):
                dma("pool", ptok[:, c, :], p_d[l, s, t0 + c * 128:t0 + (c + 1) * 128, :], [],
                    ["ptok"] + (UNION_KEYS_MIX if c == 0 else []), ("pt", 0))
            for kk in range(2):
                b = bank("mm")
                def fn(b=b, kk=kk):
                    ins = None
                    for c in range(NCH):
                        ins = nc.tensor.transpose(out=ps[b][:, c * 128:(c + 1) * 128],
                                                  in_=ptok[:, c, kk * 128:(kk + 1) * 128], identity=ident[:, :])
                    return ins
                P.add("pe", fn, reads=["ptok", "ident"], writes=[PK(b)])
                cp("dve", pT[:, kk, :], ps[b][:, :], [PK(b)], ["pT"] + (UNION_KEYS_MIX if kk == 0 else []))
            sb_next = bank("stat")
            for m in range(KT_D):
                wg, wgkey = wload(unit("ple_w_gate", l, m), KT_D * 128)
                wp, wpkey = wload(unit("ple_w_proj", l, m), 2 * 128)
                bg = bank("mm"); bp = bank("mm")
                mm_group(bg, wg, wgkey, KT_D, lambda k: hT[:, k, :], ["hT"])
                mm_group(bp, wp, wpkey, 2, lambda k: pT[:, k, :], ["pT"])
                gt, gtkey = tmp()
                act_op(gt[:, :], ps[bg][:, :], AF.Sigmoid, [PK(bg)], [gtkey])
                tt("dve", gt[:, :], gt[:, :], ps[bp][:, :], ALU.mult, [gtkey, PK(bp)], [gtkey])
                if "nople" not in debug:
                    tt("pool", xT[:, m, :], xT[:, m, :], gt[:, :], ALU.add, [("xT", m), gtkey], [("xT", m)])
                stat_piece(sb_next, m)
            sb_mix = sb_next

        finish_rstd(sb_mix, rstd_x, "rstd_x", float(D))
        for k in range(KT_D):
            stt("dve", xT[:, k, :], xT[:, k, :], fin_g[:, k:k + 1], rstd_x[:, :], ALU.mult, ALU.mult,
                [("xT", k), "fin_g", "rstd_x"], [("xT", k)])
        for c in range(NCH):
            st = iostage[c % 2]; skey = f"io{c % 2}"
            for kq in range(4):
                b = bank("mm")
                def fn(b=b, kq=kq, c=c):
                    ins = None
                    for kk in range(4):
                        k = kq * 4 + kk
                        ins = nc.tensor.transpose(out=ps[b][:, kk * 128:(kk + 1) * 128],
                                                  in_=xT[:, k, c * 128:(c + 1) * 128], identity=ident[:, :])
                    return ins
                P.add("pe", fn, reads=[("xT", kq * 4 + kk_) for kk_ in range(4)] + ["ident"], writes=[PK(b)])
                cp("act" if kq % 2 else "dve", st[:, kq * 512:(kq + 1) * 512], ps[b][:, :], [PK(b)],
                   [skey] + ((UNION_KEYS_MIX + UNION_KEYS_FFN) if (c < 2 and kq == 0) else []))
            dma("pool", out_d[s, t0 + c * 128:t0 + (c + 1) * 128, :], st[:, :], [skey], ["OUT"], ("io", c % 2))

    P.add("pool", lambda: None, reads=[f"io0", "io1"], writes=["OUT", "io0", "io1", "mixT", "zf", "vtok", "uT", "xsb", "rstd_a", "rstd_b", "zb"])

    P.finalize_counts()
    n_dma = len(P.dma_keys)
    sem_cms = [nc.semaphore(f"e_{e}") for e in ENGS] + [nc.semaphore(f"d_{i}") for i in range(n_dma)]
    sems = [cm.__enter__() for cm in sem_cms]
    eng_sems = {e: sems[i] for i, e in enumerate(ENGS)}
    dma_sems = {i: sems[len(ENGS) + i] for i in range(n_dma)}
    with nc.Block() as block:
        @block.sync
        def _(e):
            P.emit_engine("sp", nc.sync, eng_sems, dma_sems)

        @block.scalar
        def _(e):
            P.emit_engine("act", nc.scalar, eng_sems, dma_sems)

        @block.vector
        def _(e):
            P.emit_engine("dve", nc.vector, eng_sems, dma_sems)

        @block.gpsimd
        def _(e):
            P.emit_engine("pool", nc.gpsimd, eng_sems, dma_sems)

        @block.tensor
        def _(e):
            P.emit_engine("pe", nc.tensor, eng_sems, dma_sems)
    for cm in reversed(sem_cms):
        cm.__exit__(None, None, None)
    for cm in reversed(ps_cms):
        cm.__exit__(None, None, None)
    arena_cm.__exit__(None, None, None)
    return nc


_WNAMES = ["mix_norm", "w_in", "sgu_norm", "sgu_w", "sgu_b", "s5_lam_re", "s5_lam_im", "s5_log_dt",
           "s5_b_re", "s5_b_im", "s5_c_re", "s5_c_im", "s5_d", "s5_glu_w", "s5_glu_b", "out_norm_a",
           "out_norm_b", "w_out", "ffn_norm", "ffn_w_up", "ffn_conv_w", "ffn_conv_b", "ffn_w_down",
           "ple_norm", "ple_w_gate", "ple_w_proj", "final_norm"]


def kernel(**inputs):
    n_cores = 8
    per = NB // n_cores
    nc = build_program()
    shared = {k: np.ascontiguousarray(np.asarray(inputs[k], dtype=np.float32)) for k in _WNAMES}
    x = np.asarray(inputs["x"], dtype=np.float32)
    p = np.asarray(inputs["p"], dtype=np.float32)
    in_maps = []
    for c in range(n_cores):
        m = dict(shared)
        m["x"] = np.ascontiguousarray(x[c * per:(c + 1) * per])
        m["p"] = np.ascontiguousarray(p[:, c * per:(c + 1) * per])
        in_maps.append(m)
    res = run_bass_kernel_spmd(nc, in_maps, core_ids=list(range(n_cores)))
    out = np.concatenate([np.asarray(r["out"]) for r in res.results], axis=0)
    return out.astype(np.float32, copy=False)
```

```python
import numpy as np
import concourse.bass as bass
import concourse.mybir as mybir
from concourse.bass_utils import run_bass_kernel_spmd

F32 = mybir.dt.float32
BF16 = mybir.dt.bfloat16
I32 = mybir.dt.int32
AF = mybir.ActivationFunctionType
ALU = mybir.AluOpType

D = 2048
SEQ = 2048
NB = 16
DEPTH = 2
D_PLE = 256
D_SGU = 1024
D_SSM = 1024
D_FF = 5632
NG = 64
EPS = 1e-6
NT = 512
NCH = NT // 128
KT_D = D // 128
TWO_PI = float(2 * np.pi)
SHR = 1.0 - 1e-6

ENGS = ("pe", "act", "dve", "pool", "sp")


class _Op:
    __slots__ = ("eng", "fn", "needs_inc", "waits", "dma", "count", "small")


class Prog:
    def __init__(self):
        self.ops = {e: [] for e in ENGS}
        self.lastw = {}
        self.readers = {}
        self.dma_keys = {}
        self.small = False

    def add(self, eng, fn, reads=(), writes=(), dma=None):
        op = _Op()
        op.eng = eng; op.fn = fn; op.needs_inc = False; op.dma = None; op.count = 0
        op.small = self.small
        deps = []
        for k in reads:
            t = self.lastw.get(k)
            if t is not None:
                deps.append(t)
        for k in writes:
            t = self.lastw.get(k)
            if t is not None:
                deps.append(t)
            deps.extend(self.readers.get(k, {}).values())
        if dma is not None:
            ent = self.dma_keys.setdefault(dma, [len(self.dma_keys), 0, None])
            if ent[2] is not None:
                deps.append(ent[2])
            ent[1] += 1
            tok = ("dma", ent[0], 16 * ent[1])
            ent[2] = tok
            op.dma = ent[0]
        else:
            tok = ("eng", eng, op)
        w = []
        for d in deps:
            if d[0] == "eng":
                if d[1] == eng and dma is None and not d[2].small:
                    continue
                d[2].needs_inc = True
            w.append(d)
        op.waits = w
        self.ops[eng].append(op)
        for k in writes:
            self.lastw[k] = tok
            self.readers[k] = {}
        for k in reads:
            rk = (tok[0], tok[1]) if tok[0] == "eng" else tok
            self.readers.setdefault(k, {})[rk] = tok
        return tok

    def finalize_counts(self):
        for e in ENGS:
            c = 0
            for op in self.ops[e]:
                if op.needs_inc:
                    c += 1
                    op.count = c

    def emit_engine(self, eng, engine, eng_sems, dma_sems):
        waited = {}
        for op in self.ops[eng]:
            for d in op.waits:
                if d[0] == "eng":
                    key = ("e", d[1]); sem = eng_sems[d[1]]; val = d[2].count
                else:
                    key = ("d", d[1]); sem = dma_sems[d[1]]; val = d[2]
                if waited.get(key, 0) >= val:
                    continue
                waited[key] = val
                engine.wait_ge(sem, val)
            ins = op.fn()
            if ins is None:
                continue
            if op.dma is not None:
                ins.then_inc(dma_sems[op.dma], 16)
            elif op.needs_inc:
                ins.then_inc(eng_sems[eng], 1)


def _weight_specs():
    return [
        ("w_in", D, 3 * 1024),
        ("s5_glu_w", D_SSM, D_SSM),
        ("w_out", D, D),
        ("ffn_w_up", D, 2 * D_FF),
        ("ffn_w_down", D_FF, D),
        ("ple_w_gate", D, D),
        ("ple_w_proj", D_PLE, D),
    ]


def build_program(n_tiles_per_seq=SEQ // NT, n_seq=2, depth=DEPTH, debug=None):
    nc = bass.Bass("TRN2", target_bir_lowering=False)
    P = Prog()
    debug = debug or set()
    L = depth

    def dram_in(name, shape):
        return nc.dram_tensor(name, list(shape), F32, kind="ExternalInput").ap()

    x_d = dram_in("x", [n_seq, SEQ, D])
    p_d = dram_in("p", [DEPTH, n_seq, SEQ, D_PLE])
    mix_norm_d = dram_in("mix_norm", [DEPTH, D])
    w_in_d = dram_in("w_in", [DEPTH, D, 3072])
    sgu_norm_d = dram_in("sgu_norm", [DEPTH, D_SGU])
    sgu_w_d = dram_in("sgu_w", [DEPTH, 8, 128, 128])
    sgu_b_d = dram_in("sgu_b", [DEPTH, 8, 128])
    lam_re_d = dram_in("s5_lam_re", [DEPTH, NG, 64])
    lam_im_d = dram_in("s5_lam_im", [DEPTH, NG, 64])
    log_dt_d = dram_in("s5_log_dt", [DEPTH, NG])
    b_re_d = dram_in("s5_b_re", [DEPTH, NG, 64, 16])
    b_im_d = dram_in("s5_b_im", [DEPTH, NG, 64, 16])
    c_re_d = dram_in("s5_c_re", [DEPTH, NG, 16, 64])
    c_im_d = dram_in("s5_c_im", [DEPTH, NG, 16, 64])
    s5_d_d = dram_in("s5_d", [DEPTH, D_SSM])
    glu_w_d = dram_in("s5_glu_w", [DEPTH, D_SSM, D_SSM])
    glu_b_d = dram_in("s5_glu_b", [DEPTH, D_SSM])
    out_norm_a_d = dram_in("out_norm_a", [DEPTH, D_SGU])
    out_norm_b_d = dram_in("out_norm_b", [DEPTH, D_SSM])
    w_out_d = dram_in("w_out", [DEPTH, D, D])
    ffn_norm_d = dram_in("ffn_norm", [DEPTH, D])
    ffn_up_d = dram_in("ffn_w_up", [DEPTH, D, 2 * D_FF])
    conv_w_d = dram_in("ffn_conv_w", [DEPTH, 3, 2 * D_FF])
    conv_b_d = dram_in("ffn_conv_b", [DEPTH, 2 * D_FF])
    ffn_down_d = dram_in("ffn_w_down", [DEPTH, D_FF, D])
    ple_norm_d = dram_in("ple_norm", [DEPTH, D])
    ple_gate_d = dram_in("ple_w_gate", [DEPTH, D, D])
    ple_proj_d = dram_in("ple_w_proj", [DEPTH, D_PLE, D])
    final_norm_d = dram_in("final_norm", [D])
    wsrc = {"w_in": w_in_d, "s5_glu_w": glu_w_d, "w_out": w_out_d, "ffn_w_up": ffn_up_d,
            "ffn_w_down": ffn_down_d, "ple_w_gate": ple_gate_d, "ple_w_proj": ple_proj_d}

    out_d = nc.dram_tensor("out", [n_seq, SEQ, D], F32, kind="ExternalOutput").ap()

    wscr = {}
    for l in range(L):
        for name, K, M in _weight_specs():
            wscr[(name, l)] = nc.dram_tensor(f"ws_{name}_{l}", [M // 128, 128, (K // 128) * 128], BF16).ap()
    dkind = "ExternalOutput" if "dump" in (debug or ()) else "Internal"
    s5w_scr = nc.dram_tensor("s5w_scr", [L, NG, 128, 4 * 128], BF16, kind=dkind).ap()
    tab_scr = nc.dram_tensor("tab_scr", [L, NG, 128, 2, SEQ], BF16, kind=dkind).ap()

    ARENA_BYTES = 212800
    arena_cm = nc.sbuf_tensor("arena", [128, ARENA_BYTES // 4], F32)
    arena = arena_cm.__enter__()
    cur = [0]

    def carve_at(off, nbytes, dtype, pattern=None, **kw):
        assert off % 4 == 0 and nbytes % 4 == 0
        assert off + nbytes <= ARENA_BYTES, (off, nbytes)
        v = arena[:, off // 4:(off + nbytes) // 4]
        if dtype != F32:
            v = v.bitcast(dtype)
        if pattern:
            v = v.rearrange(pattern, **kw)
        return v

    def carve(nbytes, dtype, pattern=None, **kw):
        off = cur[0]
        cur[0] += (nbytes + 31) // 32 * 32
        return carve_at(off, nbytes, dtype, pattern, **kw)

    ident = carve(512, F32)
    ones_b = carve(256, BF16)
    rows_b = carve(256 + 2 * 2048, BF16)
    ccol = carve(64, F32)
    maskcol = carve(32, F32)
    NVEC = 80
    gvec = carve(L * NVEC * 4, F32, "p (l v) -> p l v", l=L)
    fin_g = carve(64, F32)
    convc = carve(L * 4 * 88 * 4, F32, "p (l a j) -> p l a j", l=L, a=4)
    sgu_g = carve(L * 2048, BF16, "p (l c) -> p l c", l=L)
    wmT = carve(L * 2048, BF16, "p (l h t) -> p l h t", l=L, h=8)
    amag = carve(L * 256, F32, "p (l g) -> p l g", l=L)
    cc = carve(L * 88 * 2 * 4, F32, "p (l j t) -> p l j t", l=L, t=2)
    s5c = carve(L * 256, F32, "p (l g) -> p l g", l=L)
    const_end = cur[0]

    xT = carve(32768, F32, "p (k n) -> p k n", n=NT)
    hT = carve(16384, BF16, "p (k n) -> p k n", n=NT)
    mixT = carve(16384, BF16, "p (k n) -> p k n", n=NT)
    tmpf = [carve(2048, F32) for _ in range(4)]
    sqb = [carve(1024, BF16) for _ in range(2)]
    rstd_x = carve(2048, F32)
    rstd_a = carve(2048, F32)
    rstd_b = carve(2048, F32)
    smallc = carve(128, F32)
    NWB = 5
    WB_BYTES = 22 * 256
    wbuf = [carve(WB_BYTES, BF16) for _ in range(NWB)]
    NTB = 4
    gring = [carve(3072, BF16) for _ in range(NTB)]
    U0 = cur[0]
    UNION_BYTES = 69632
    cur[0] += UNION_BYTES
    main_end = cur[0]
    assert main_end <= ARENA_BYTES, main_end
    uT = carve_at(U0 + 0, 8192, BF16, "p (k n) -> p k n", n=NT)
    vtok = carve_at(U0 + 8192, 8192, BF16, "p (c d) -> p c d", c=NCH)
    xsb = carve_at(U0 + 16384, 8192, BF16, "p (k n) -> p k n", n=NT)
    zf = carve_at(U0 + 24576, 16384, F32, "p (k n) -> p k n", n=NT)
    zb = carve_at(U0 + 40960, 8192, BF16, "p (k n) -> p k n", n=NT)
    NS5 = 3
    S5B = U0 + 49152
    s5pb = [carve_at(S5B + i * 2048, 2048, BF16) for i in range(2)]
    s5t12 = [carve_at(S5B + 4096 + i * 2048, 2048, BF16) for i in range(2)]
    s5g = [carve_at(S5B + 8192 + i * 2048, 2048, F32) for i in range(NS5)]
    s5g12 = [carve_at(S5B + 8192 + NS5 * 2048 + i * 2048, 2048, BF16, "p (a n) -> p a n", a=2) for i in range(NS5)]
    assert S5B + 8192 + NS5 * 4096 <= U0 + UNION_BYTES
    hid = carve_at(U0 + 0, 45056, BF16, "p (k n) -> p k n", n=NT)
    abuf = [carve_at(U0 + 45056 + i * 2064, 2056, F32) for i in range(4)]
    ptok = carve_at(U0 + 53312, 4096, F32, "p (c d) -> p c d", c=NCH)
    pT = carve_at(U0 + 57408, 2048, BF16, "p (k n) -> p k n", n=NT)
    iostage = [carve_at(U0 + i * 8192, 8192, F32) for i in range(2)]
    UNION_KEYS_MIX = (["uT", "vtok", "xsb", "zf", "zb", "s5pb0", "s5pb1", "s5t120", "s5t121"]
                      + [f"s5g{i}" for i in range(3)] + [f"s5g12{i}" for i in range(3)])
    UNION_KEYS_FFN = ["hid", "ab0", "ab1", "ab2", "ab3", "ptok", "pT", "io0", "io1"]
    mix_first = {}

    def mixw(l_, s_, ti_, key):
        k = (s_, ti_, l_, key)
        if k in mix_first:
            return []
        mix_first[k] = True
        return UNION_KEYS_FFN

    PR0 = const_end
    pro = {}
    pcur = [PR0]

    def pcarve(nbytes, dtype, pattern=None, **kw):
        off = pcur[0]
        pcur[0] += (nbytes + 31) // 32 * 32
        return carve_at(off, nbytes, dtype, pattern, **kw)

    ps_cms = [nc.psum_tensor("psall", [128, 8 * 512], F32)]
    psall = ps_cms[0].__enter__()
    ps = [psall[:, i * 512:(i + 1) * 512] for i in range(8)]
    bank_rr = {"mm": [0, 1, 2, 3], "acc": [6], "aux": [6, 7], "stat": [7], "s5p": [0, 1, 2, 3], "mm2": [4, 5]}
    bank_ctr = {k_: 0 for k_ in bank_rr}

    def bank(kind):
        lst = bank_rr[kind]
        b = lst[bank_ctr[kind] % len(lst)]
        bank_ctr[kind] += 1
        return b

    def PK(b):
        return ("ps", b)

    E = {"pe": nc.tensor, "act": nc.scalar, "dve": nc.vector, "pool": nc.gpsimd, "sp": nc.sync}

    ew_rr = [0]

    def act_op(out, in_, func, reads, writes, scale=1.0, bias=None, accum_out=None):
        def fn():
            kw = {}
            if bias is not None:
                kw["bias"] = bias
            if accum_out is not None:
                kw["accum_out"] = accum_out
            return nc.scalar.activation(out=out, in_=in_, func=func, scale=scale, **kw)
        return P.add("act", fn, reads=reads, writes=writes)

    def tt(eng, out, in0, in1, op, reads, writes):
        e = E[eng]
        return P.add(eng, lambda: e.tensor_tensor(out=out, in0=in0, in1=in1, op=op), reads=reads, writes=writes)

    def ts(eng, out, in0, s1, s2, op0, op1, reads, writes):
        e = E[eng]
        if op1 is None:
            return P.add(eng, lambda: e.tensor_scalar(out=out, in0=in0, scalar1=s1, scalar2=None, op0=op0),
                         reads=reads, writes=writes)
        return P.add(eng, lambda: e.tensor_scalar(out=out, in0=in0, scalar1=s1, scalar2=s2, op0=op0, op1=op1),
                     reads=reads, writes=writes)

    def stt(eng, out, in0, scalar, in1, op0, op1, reads, writes):
        e = E[eng]
        return P.add(eng, lambda: e.scalar_tensor_tensor(out=out, in0=in0, scalar=scalar, in1=in1, op0=op0, op1=op1),
                     reads=reads, writes=writes)

    def cp(eng, out, in_, reads, writes):
        if eng == "act":
            return P.add("act", lambda: nc.scalar.copy(out=out, in_=in_), reads=reads, writes=writes)
        e = E[eng]
        return P.add(eng, lambda: e.tensor_copy(out=out, in_=in_), reads=reads, writes=writes)

    def memset(eng, ap, val, writes):
        e = E[eng]
        return P.add(eng, lambda: e.memset(ap, val), writes=writes)

    def dma(eng, out, in_, reads, writes, key, slow=False):
        e = E[eng]
        if slow:
            return P.add(eng, lambda: e.dma_start(out=out, in_=in_, allow_slow_non_contiguous=True),
                         reads=reads, writes=writes, dma=key)
        return P.add(eng, lambda: e.dma_start(out=out, in_=in_), reads=reads, writes=writes, dma=key)

    def transpose(out, in_, reads, writes):
        return P.add("pe", lambda: nc.tensor.transpose(out=out, in_=in_, identity=ident[:, :]),
                     reads=list(reads) + ["ident"], writes=writes)

    P.small = True
    memset("pool", ident[:, :], 1.0, ["ident"])
    P.add("pool", lambda: nc.gpsimd.affine_select(out=ident[:, :], in_=ident[:, :], pattern=[[1, 128]],
                                                  compare_op=ALU.is_ge, fill=0.0, base=0, channel_multiplier=-1),
          writes=["ident"])
    P.add("pool", lambda: nc.gpsimd.affine_select(out=ident[:, :], in_=ident[:, :], pattern=[[-1, 128]],
                                                  compare_op=ALU.is_ge, fill=0.0, base=0, channel_multiplier=1),
          writes=["ident"])
    memset("pool", ones_b[:, :], 1.0, ["ones_b"])
    memset("pool", rows_b[:, 0:128], 1.0, ["rows_b"])
    memset("pool", ccol[:, 0:1], EPS, ["ccol"])
    memset("pool", ccol[:, 1:2], float(np.pi * SHR), ["ccol"])
    memset("pool", ccol[:, 2:3], float(-np.pi), ["ccol"])
    memset("pool", ccol[:, 3:4], float(-np.pi / 2), ["ccol"])
    memset("pool", ccol[:, 4:5], 0.0, ["ccol"])
    memset("pool", ccol[:, 5:6], float(np.pi / 2 * SHR), ["ccol"])
    memset("pool", maskcol[:, :], 1.0, ["maskcol"])
    P.add("pool", lambda: nc.gpsimd.affine_select(out=maskcol[:, :], in_=maskcol[:, :], pattern=[[-16, 8]],
                                                  compare_op=ALU.is_ge, fill=0.0, base=0, channel_multiplier=1),
          writes=["maskcol"])
    P.add("pool", lambda: nc.gpsimd.affine_select(out=maskcol[:, :], in_=maskcol[:, :], pattern=[[16, 8]],
                                                  compare_op=ALU.is_ge, fill=0.0, base=15, channel_multiplier=-1),
          writes=["maskcol"])
    memset("pool", cc[:, :, :, :], 0.0, ["cc"])
    memset("pool", s5c[:, :, :], 0.0, [("s5c", l_, g_) for l_ in range(L) for g_ in range(NG)])

    P.small = False
    def colload(dst, src_vec, key):
        dma("pool", dst, src_vec.rearrange("(k p) -> p k", p=128), [], [key], ("cst", 0), slow=True)

    for l in range(L):
        colload(gvec[:, l, 0:16], mix_norm_d[l], "gvec")
        colload(gvec[:, l, 16:32], ffn_norm_d[l], "gvec")
        colload(gvec[:, l, 32:48], ple_norm_d[l], "gvec")
        colload(gvec[:, l, 48:56], out_norm_a_d[l], "gvec")
        colload(gvec[:, l, 56:64], out_norm_b_d[l], "gvec")
        colload(gvec[:, l, 64:72], s5_d_d[l], "gvec")
        colload(gvec[:, l, 72:80], glu_b_d[l], "gvec")
        for a in range(3):
            colload(convc[:, l, a, :], conv_w_d[l, a], "convc")
        colload(convc[:, l, 3, :], conv_b_d[l], "convc")
    colload(fin_g[:, :], final_norm_d, "fin_g")

    pr_f = [pcarve(4096, F32) for _ in range(2)]
    for l in range(L):
        dma("pool", pr_f[0][:, :], sgu_norm_d[l].partition_broadcast(128), [], ["prf0"], ("cst", 0))
        cp("dve", sgu_g[:, l, :], pr_f[0][:, :], ["prf0"], ["sgu_g"])
    for l in range(L):
        sb_row = pr_f[0]
        dma("pool", sb_row[0:1, :], sgu_b_d[l:l + 1].rearrange("l h t -> l (h t)"), [], ["prf0"], ("cst", 0))
        cp("dve", rows_b[0:1, 128 + l * 1024:128 + (l + 1) * 1024], sb_row[0:1, :], ["prf0"], ["rows_b"])
        for h in range(8):
            wst = pr_f[1]
            dma("pool", wst[:, 0:128], sgu_w_d[l, h], [], ["prf1"], ("cst", 0))
            b = bank("aux")
            transpose(ps[b][:, 0:128], wst[:, 0:128], ["prf1"], [PK(b)])
            cp("dve", wst[:, 128:256], ps[b][:, 0:128], [PK(b)], ["prf1b"])
            P.add("pool", lambda wst=wst: nc.gpsimd.affine_select(
                out=wst[:, 128:256], in_=wst[:, 128:256], pattern=[[1, 128]], compare_op=ALU.is_ge, fill=0.0,
                base=0, channel_multiplier=-1), reads=["prf1b"], writes=["prf1b"])
            cp("dve", wmT[:, l, h, :], wst[:, 128:256], ["prf1b"], ["wmT"])

    def pc64():
        return pcarve(256, F32)

    iota_t = pcarve(SEQ * 4, F32)
    P.add("pool", lambda: nc.gpsimd.iota(iota_t[:, :], pattern=[[1, SEQ]], base=0, channel_multiplier=0,
                                         allow_small_or_imprecise_dtypes=True), writes=["iota"])
    LR, LI, DT, TH, T1, T2, T3, T4, ABR, ABI, QR, QI, QA, QB, QA2, QB2 = [pc64() for _ in range(16)]
    T1i = pcarve(256, I32)
    BR2 = pcarve(4096, F32, "p (g c) -> p g c", c=16)
    BI2 = pcarve(4096, F32, "p (g c) -> p g c", c=16)
    W1 = pcarve(4096, F32, "p (g c) -> p g c", c=16)
    W2 = pcarve(4096, F32, "p (g c) -> p g c", c=16)
    WT = pcarve(4096, F32, "p (g c) -> p g c", c=16)
    CC1 = pcarve(512, F32)
    CC2 = pcarve(512, F32)
    CCT = pcarve(512, F32)
    pbst = [pcarve(4096, BF16, "p (a n) -> p a n", a=16) for _ in range(2)]
    pcst = [pcarve(4096, BF16, "p (a n) -> p a n", a=16) for _ in range(2)]
    phs = [pcarve(SEQ * 4, F32) for _ in range(2)]
    phk = pcarve(SEQ * 4, I32)
    phf = pcarve(SEQ * 4, F32)
    tabst = [pcarve(SEQ * 2 * 2, BF16, "p (a n) -> p a n", a=2) for _ in range(2)]
    s5_end = pcur[0]

    def range_reduce(eng, x, kint, kf, kkey, kfkey, key):
        ts(eng, kint, x, float(1.0 / TWO_PI), None, ALU.mult, None, [key], [kkey])
        cp(eng, kf, kint, [kkey], [kfkey])
        stt(eng, x, kf, -TWO_PI, x, ALU.mult, ALU.add, [kfkey, key], [key])
        ts(eng, kf, x, 0.0, TWO_PI, ALU.is_lt, ALU.mult, [key], [kfkey])
        tt(eng, x, x, kf, ALU.add, [key, kfkey], [key])

    def sin_of(out, r, reads, writes):
        return act_op(out, r, AF.Sin, reads, writes, scale=-SHR, bias=ccol[:, 1:2])

    def cos_of(out, r, tmp, reads, writes, tkey):
        act_op(tmp, r, AF.Abs, reads, [tkey], scale=1.0, bias=ccol[:, 2:3])
        return act_op(out, tmp, AF.Sin, [tkey], writes, scale=SHR, bias=ccol[:, 3:4])

    def s5_prologue():
      for l in range(L):
          P.small = True
          for half in range(2):
              hs = slice(half * 64, half * 64 + 64)
              dma("pool", LR[hs, :], lam_re_d[l].rearrange("g n -> n g"), [], ["LR"], ("cst", 0), slow=True)
              dma("pool", LI[hs, :], lam_im_d[l].rearrange("g n -> n g"), [], ["LI"], ("cst", 0), slow=True)
              dma("pool", BR2[hs, :, :], b_re_d[l].rearrange("g n c -> n g c"), [], ["BR2"], ("cst", 0))
              dma("pool", BI2[hs, :, :], b_im_d[l].rearrange("g n c -> n g c"), [], ["BI2"], ("cst", 0))
          dma("pool", DT[:, :], log_dt_d[l].partition_broadcast(128), [], ["DT"], ("cst", 0))
          act_op(DT[:, :], DT[:, :], AF.Exp, ["DT"], ["DT"])
          tt("dve", T1[:, :], LR[:, :], DT[:, :], ALU.mult, ["LR", "DT"], ["T1"])
          act_op(amag[:, l, :], T1[:, :], AF.Exp, ["T1"], ["amag"])
          tt("dve", TH[:, :], LI[:, :], DT[:, :], ALU.mult, ["LI", "DT"], ["TH"])
          cp("dve", T2[:, :], TH[:, :], ["TH"], ["T2"])
          range_reduce("dve", T2[:, :], T1i[:, :], T3[:, :], "T1i", "T3", "T2")
          sin_of(ABI[:, :], T2[:, :], ["T2", "ccol"], ["ABI"])
          cos_of(ABR[:, :], T2[:, :], T4[:, :], ["T2", "ccol"], ["ABR"], "T4")
          tt("dve", ABR[:, :], ABR[:, :], amag[:, l, :], ALU.mult, ["ABR", "amag"], ["ABR"])
          tt("dve", ABI[:, :], ABI[:, :], amag[:, l, :], ALU.mult, ["ABI", "amag"], ["ABI"])
          tt("dve", T1[:, :], LR[:, :], LR[:, :], ALU.mult, ["LR"], ["T1"])
          tt("dve", T3[:, :], LI[:, :], LI[:, :], ALU.mult, ["LI"], ["T3"])
          tt("dve", T1[:, :], T1[:, :], T3[:, :], ALU.add, ["T1", "T3"], ["T1"])
          P.add("dve", lambda: nc.vector.reciprocal(out=T1[:, :], in_=T1[:, :]), reads=["T1"], writes=["T1"])
          ts("dve", T3[:, :], ABR[:, :], -1.0, None, ALU.add, None, ["ABR"], ["T3"])
          tt("dve", QR[:, :], T3[:, :], LR[:, :], ALU.mult, ["T3", "LR"], ["QR"])
          tt("dve", T4[:, :], ABI[:, :], LI[:, :], ALU.mult, ["ABI", "LI", "T4"], ["T4"])
          tt("dve", QR[:, :], QR[:, :], T4[:, :], ALU.add, ["QR", "T4"], ["QR"])
          tt("dve", QR[:, :], QR[:, :], T1[:, :], ALU.mult, ["QR", "T1"], ["QR"])
          tt("dve", QI[:, :], ABI[:, :], LR[:, :], ALU.mult, ["ABI", "LR"], ["QI"])
          tt("dve", T4[:, :], T3[:, :], LI[:, :], ALU.mult, ["T3", "LI"], ["T4"])
          tt("dve", QI[:, :], QI[:, :], T4[:, :], ALU.subtract, ["QI", "T4"], ["QI"])
          tt("dve", QI[:, :], QI[:, :], T1[:, :], ALU.mult, ["QI", "T1"], ["QI"])
          lo, hi = slice(0, 64), slice(64, 128)
          cp("dve", QA[lo, :], QR[lo, :], ["QR"], ["QA"])
          cp("dve", QA[hi, :], QI[hi, :], ["QI"], ["QA"])
          ts("dve", QB[lo, :], QI[lo, :], -1.0, None, ALU.mult, None, ["QI"], ["QB"])
          cp("dve", QB[hi, :], QR[hi, :], ["QR"], ["QB"])
          cp("dve", QA2[lo, :], QI[lo, :], ["QI"], ["QA2"])
          ts("dve", QA2[hi, :], QR[hi, :], -1.0, None, ALU.mult, None, ["QR"], ["QA2"])
          cp("dve", QB2[lo, :], QR[lo, :], ["QR"], ["QB2"])
          cp("dve", QB2[hi, :], QI[hi, :], ["QI"], ["QB2"])

          P.small = False

          def bc(q):
              return q[:, :].unsqueeze(2).broadcast_to([128, NG, 16])
          tt("dve", W1[:, :, :], BR2[:, :, :], bc(QA), ALU.mult, ["BR2", "QA"], ["W1"])
          tt("dve", WT[:, :, :], BI2[:, :, :], bc(QB), ALU.mult, ["BI2", "QB"], ["WT"])
          tt("dve", W1[:, :, :], W1[:, :, :], WT[:, :, :], ALU.add, ["W1", "WT"], ["W1"])
          tt("dve", W2[:, :, :], BR2[:, :, :], bc(QA2), ALU.mult, ["BR2", "QA2"], ["W2"])
          tt("dve", WT[:, :, :], BI2[:, :, :], bc(QB2), ALU.mult, ["BI2", "QB2"], ["WT"])
          tt("dve", W2[:, :, :], W2[:, :, :], WT[:, :, :], ALU.add, ["W2", "WT"], ["W2"])
          for mt in range(8):
              st = pbst[mt % 2]
              skey = f"pbst{mt % 2}"
              for wi, Wx in enumerate((W1, W2)):
                  b = bank("aux")
                  transpose(ps[b][:, 0:128], Wx[:, mt * 8:(mt + 1) * 8, :].rearrange("p g c -> p (g c)"),
                            ["W1", "W2"], [PK(b)])
                  for j in range(8):
                      ts("dve" if j % 2 == 0 else "act" if False else "dve", st[:, 2 * j + wi, :], ps[b][:, 0:128],
                         maskcol[:, j:j + 1], None, ALU.mult, None, [PK(b), "maskcol"], [skey])
              dma("act", s5w_scr[l, mt * 8:(mt + 1) * 8, :, 0:256].rearrange("g p (w n) -> p g w n", w=2),
                  st[:, :, :].rearrange("p (g w) n -> p g w n", w=2), [skey], [], ("pbst", mt % 2))
              g0 = mt * 8
              dma("pool", CC1[:, 0:64], c_re_d[l, g0:g0 + 8].rearrange("g c n -> (g c) n"), [], ["CC1"], ("cst", 0))
              dma("pool", CC2[:, 64:128], c_re_d[l, g0:g0 + 8].rearrange("g c n -> (g c) n"), [], ["CC2"], ("cst", 0))
              dma("pool", CC1[:, 64:128], c_im_d[l, g0:g0 + 8].rearrange("g c n -> (g c) n"), [], ["CC1"], ("cst", 0))
              dma("pool", CC2[:, 0:64], c_im_d[l, g0:g0 + 8].rearrange("g c n -> (g c) n"), [], ["CC2"], ("cst", 0))
              ts("dve", CC1[:, 64:128], CC1[:, 64:128], -1.0, None, ALU.mult, None, ["CC1"], ["CC1"])
              ts("dve", CC2[:, :], CC2[:, :], -1.0, None, ALU.mult, None, ["CC2"], ["CC2"])
              cs = pcst[mt % 2]
              ckey = f"pcst{mt % 2}"
              memset("pool", cs[:, :, :], 0.0, [ckey])
              for wi, Cx in enumerate((CC1, CC2)):
                  b = bank("aux")
                  transpose(ps[b][:, 0:128], Cx[:, :], ["CC1", "CC2"], [PK(b)])
                  for j in range(8):
                      cp("dve", cs[:, 2 * j + wi, 16 * j:16 * j + 16], ps[b][:, 16 * j:16 * j + 16], [PK(b)], [ckey])
              dma("act", s5w_scr[l, mt * 8:(mt + 1) * 8, :, 256:512].rearrange("g p (w n) -> p g w n", w=2),
                  cs[:, :, :].rearrange("p (g w) n -> p g w n", w=2), [ckey], [], ("pcst", mt % 2))
              yield
          for g in range(NG):
              i = g % 2
              ph = phs[i]; pkey = f"ph{i}"
              tb = tabst[i]; tkey = f"tabst{i}"
              P.add("dve", lambda g=g: nc.vector.tensor_scalar(out=phk[:, :], in0=iota_t[:, :], scalar1=TH[:, g:g + 1],
                                                             scalar2=float(1.0 / TWO_PI), op0=ALU.mult, op1=ALU.mult),
                    reads=["iota", "TH"], writes=["phk"])
              ts("dve", phf[:, :], phk[:, :], -TWO_PI, None, ALU.mult, None, ["phk"], ["phf"])
              stt("dve", ph[:, :], iota_t[:, :], TH[:, g:g + 1], phf[:, :], ALU.mult, ALU.add, ["iota", "TH", "phf"], [pkey])
              act_op(tb[:, 1, :], ph[:, :], AF.Sin, [pkey], [tkey], scale=SHR)
              act_op(ph[:, :], ph[:, :], AF.Abs, [pkey], [pkey])
              act_op(tb[:, 0, :], ph[:, :], AF.Sin, [pkey, "ccol"], [tkey], scale=-SHR, bias=ccol[:, 5:6])
              dma("act", tab_scr[l, g], tb[:, :, :], [tkey], [], ("tabst", i))
              yield

    pcur[0] = s5_end
    CW = 1408
    NST = 2
    stin = [pcarve(4 * CW * 4, F32, "p (a c) -> p a c", a=4) for _ in range(NST)]
    stout = [pcarve(4 * CW * 2, BF16, "p (m a c) -> p m a c", a=4, c=128) for _ in range(NST)]
    assert pcur[0] <= ARENA_BYTES, pcur[0]
    cast_engs = ["act", "pool"]

    def conv_gen():
        cvt = 0
        pending = None
        for l in range(L):
            for name, K, M in _weight_specs():
                src = wsrc[name][l]
                KT = K // 128
                dst = wscr[(name, l)].rearrange("m p (k c) -> m p k c", c=128)
                for c0 in range(0, M, CW):
                    cw = min(CW, M - c0)
                    mw = cw // 128
                    for kq in range(0, KT, 4):
                        na = min(4, KT - kq)
                        i = cvt % NST
                        cvt += 1
                        si, so = stin[i], stout[i]
                        dma("sp", si[:, 0:na, 0:cw],
                            src[kq * 128:(kq + na) * 128, c0:c0 + cw].rearrange("(a p) c -> p a c", p=128),
                            [], [f"stin{i}"], ("stin", i))

                        def finish(i=i, si=si, so=so, na=na, cw=cw, mw=mw, dst=dst, c0=c0, kq=kq, cvt=cvt):
                            for a in range(na):
                                eng = cast_engs[(cvt + a) % 2]
                                cp(eng, so[:, 0:mw, a, :], si[:, a, 0:cw].rearrange("p (m c) -> p m c", c=128),
                                   [f"stin{i}"], [f"stout{i}"])
                            dma("sp", dst[c0 // 128:c0 // 128 + mw, :, kq:kq + na, :].rearrange("m p a c -> p m a c"),
                                so[:, 0:mw, 0:na, :], [f"stout{i}"], [], ("stout", i))
                        if pending is not None:
                            pending()
                        pending = finish
                        yield
        pending()
        yield

    g5 = s5_prologue()
    gc = conv_gen()
    live5, livec = True, True
    while live5 or livec:
        if live5:
            try:
                next(g5)
            except StopIteration:
                live5 = False
        if livec:
            try:
                next(gc)
            except StopIteration:
                livec = False

    PRO_KEYS = (["prf0", "prf1", "prf1b", "iota", "LR", "LI", "DT", "TH", "T1", "T2", "T3", "T4", "ABR", "ABI",
                 "QR", "QI", "QA", "QB", "QA2", "QB2", "BR2", "BI2", "W1", "W2", "WT", "CC1", "CC2",
                 "pbst0", "pbst1", "pcst0", "pcst1", "ph0", "ph1", "phk", "phf",
                 "T1i", "tabst0", "tabst1", "stin0", "stin1", "stin2", "stout0", "stout1", "stout2"])
    XALL = [("xT", k_) for k_ in range(KT_D)]
    MAIN_KEYS = (XALL + ["hT", "mixT", "tmp0", "tmp1", "tmp2", "tmp3", "sqb0", "sqb1", "rstd_x", "rstd_a",
                  "rstd_b", "smallc", "hid", "ab0", "ab1", "ab2", "ab3", "ptok", "pT", "io0", "io1"]
                 + UNION_KEYS_MIX + [f"w{i}" for i in range(NWB)] + [f"gw{i}" for i in range(NTB)] + [f"gt{i}" for i in range(NTB)])
    for eng in ("sp", "pool", "act", "dve", "pe"):
        if eng in ("sp",):
            continue
    P.add("dve", lambda: nc.vector.memset(smallc[:, :], 0.0), reads=[], writes=PRO_KEYS + MAIN_KEYS + ["DRAMSCR"])

    wring = [0]

    def wload(src_ap, nbytes_cols):
        i = wring[0] % NWB
        wring[0] += 1
        buf = wbuf[i]
        dma("sp", buf[:, 0:nbytes_cols], src_ap, ["DRAMSCR"], [f"w{i}"], ("w", i))
        return buf, f"w{i}"

    tring = [0]

    def gload(l, g, t0):
        i = tring[0] % NTB
        tring[0] += 1
        dma("sp", gring[i][:, 0:512], s5w_scr[l, g], ["DRAMSCR"], [f"gw{i}"], ("gw", i))
        dma("sp", gring[i][:, 512:1536].rearrange("p (a n) -> p a n", a=2), tab_scr[l, g, :, :, t0:t0 + NT],
            ["DRAMSCR"], [f"gt{i}"], ("gt", i))
        return gring[i], f"gw{i}", f"gt{i}"

    tmp_ctr = [0]

    def tmp():
        i = tmp_ctr[0] % 4
        tmp_ctr[0] += 1
        return tmpf[i], f"tmp{i}"

    sq_ctr = [0]

    def sqt():
        i = sq_ctr[0] % 2
        sq_ctr[0] += 1
        return sqb[i], f"sqb{i}"

    def stat_piece(b, k, nk=KT_D):
        sq_, skey = sqt()
        act_op(sq_[:, :], xT[:, k, :], AF.Square, [("xT", k)], [skey])
        P.add("pe", lambda sq_=sq_, k=k, b=b: nc.tensor.matmul(ps[b][:, :], lhsT=ones_b[:, :], rhs=sq_[:, :],
                                                                 start=(k == 0), stop=(k == nk - 1)),
              reads=[skey, "ones_b"], writes=[PK(b)])

    STAT_LAG = 3

    def stat_lagged(b, m):
        if m - STAT_LAG >= 0:
            stat_piece(b, m - STAT_LAG)
        if m == KT_D - 1:
            for k in range(max(0, KT_D - STAT_LAG), KT_D):
                stat_piece(b, k)

    def rms_stats_x():
        b = bank("stat")
        for k in range(KT_D):
            stat_piece(b, k)
        return b

    def finish_rstd(b, rstd, rkey, dimn):
        act_op(rstd[:, :], ps[b][:, :], AF.Sqrt, [PK(b), "ccol"], [rkey], scale=1.0 / dimn, bias=ccol[:, 0:1])
        P.add("dve", lambda: nc.vector.reciprocal(out=rstd[:, :], in_=rstd[:, :]), reads=[rkey], writes=[rkey])

    def norm_to_hT(l, goff, b=None):
        if b is None:
            b = rms_stats_x()
        finish_rstd(b, rstd_x, "rstd_x", float(D))
        for k in range(KT_D):
            stt("dve", hT[:, k, :], xT[:, k, :], gvec[:, l, goff + k:goff + k + 1], rstd_x[:, :], ALU.mult, ALU.mult,
                [("xT", k), "gvec", "rstd_x"], ["hT"])

    def mm_group(b, wv, wkey, kt, rhs_fn, rkeys, cols=slice(0, NT)):
        def fn():
            ins = None
            for k in range(kt):
                ins = nc.tensor.matmul(ps[b][:, cols], lhsT=wv[:, k * 128:(k + 1) * 128], rhs=rhs_fn(k),
                                       start=(k == 0), stop=(k == kt - 1))
            return ins
        return P.add("pe", fn, reads=[wkey] + list(rkeys), writes=[PK(b)])

    def unit(name, l, m):
        return wscr[(name, l)][m]

    tiles = [(s, i) for s in range(n_seq) for i in range(n_tiles_per_seq)]
    first_mixer_write = [True]
    for (s, ti) in tiles:
        t0 = ti * NT
        for c in range(NCH):
            st = iostage[c % 2]; skey = f"io{c % 2}"
            extra_w = []
            dma("pool", st[:, :], x_d[s, t0 + c * 128:t0 + (c + 1) * 128, :], [], [skey] + UNION_KEYS_MIX + UNION_KEYS_FFN,
                ("io", c % 2))
            for kq in range(4):
                b = bank("mm")
                def fn(b=b, st=st, kq=kq):
                    ins = None
                    for kk in range(4):
                        k = kq * 4 + kk
                        ins = nc.tensor.transpose(out=ps[b][:, kk * 128:(kk + 1) * 128],
                                                  in_=st[:, k * 128:(k + 1) * 128], identity=ident[:, :])
                    return ins
                P.add("pe", fn, reads=[skey, "ident"], writes=[PK(b)])
                cp("act" if kq % 2 else "dve", xT[:, kq * 4:(kq + 1) * 4, c * 128:(c + 1) * 128],
                   ps[b].rearrange("p (k n) -> p k n", k=4), [PK(b)], [("xT", kq * 4 + kk_) for kk_ in range(4)])
        if ti == 0:
            for l in range(L):
                memset("pool", cc[:, l, :, :], 0.0, ["cc"])
                memset("pool", s5c[:, l, :], 0.0, [("s5c", l, g_) for g_ in range(NG)])

        sb_mix = None
        for l in range(L):
            norm_to_hT(l, 0, sb_mix)
            sb_mix = None
            def xs_unit(m):
                wv, wkey = wload(unit("w_in", l, 16 + m), KT_D * 128)
                b = bank("mm2")
                mm_group(b, wv, wkey, KT_D, lambda k: hT[:, k, :], ["hT"])
                cp("act", xsb[:, m, :], ps[b][:, :], [PK(b)], ["xsb"] + mixw(l, s, ti, "xsb"))

            def u_unit(m):
                wv, wkey = wload(unit("w_in", l, m), KT_D * 128)
                b = bank("mm2")
                mm_group(b, wv, wkey, KT_D, lambda k: hT[:, k, :], ["hT"])
                act_op(uT[:, m, :], ps[b][:, :], AF.Gelu_apprx_tanh, [PK(b)], ["uT"] + mixw(l, s, ti, "uT"))

            def v_unit(m):
                wv, wkey = wload(unit("w_in", l, 8 + m), KT_D * 128)
                b = bank("mm2")
                def fn(b=b, wv=wv):
                    ins = None
                    for c in range(NCH):
                        for k in range(KT_D):
                            ins = nc.tensor.matmul(ps[b][:, c * 128:(c + 1) * 128],
                                                   lhsT=hT[:, k, c * 128:(c + 1) * 128],
                                                   rhs=wv[:, k * 128:(k + 1) * 128],
                                                   start=(k == 0), stop=(k == KT_D - 1))
                    return ins
                P.add("pe", fn, reads=[wkey, "hT"], writes=[PK(b)])
                act_op(vtok[:, :, m * 128:(m + 1) * 128], ps[b].rearrange("p (c d) -> p c d", c=NCH),
                       AF.Gelu_apprx_tanh, [PK(b)], ["vtok"] + mixw(l, s, ti, "vtok"))

            def ln_unit(c):
                j, jkey = tmp()
                j2, j2key = tmp()
                sc = smallc
                act_op(j[:, :].bitcast(BF16), vtok[:, c, :], AF.Identity, ["vtok"], [jkey, "smallc"],
                       accum_out=sc[:, 0:1])
                act_op(j[:, :].bitcast(BF16), vtok[:, c, :], AF.Square, ["vtok"], [jkey, "smallc"],
                       accum_out=sc[:, 1:2])
                cp("act", sc[:, 8:9], sc[:, 1:2], ["smallc"], ["smallc"])
                P.small = True
                ts("dve", sc[:, 2:3], sc[:, 0:1], 1.0 / D_SGU, None, ALU.mult, None, ["smallc"], ["smallc"])
                tt("dve", sc[:, 3:4], sc[:, 2:3], sc[:, 2:3], ALU.mult, ["smallc"], ["smallc"])
                stt("dve", sc[:, 4:5], sc[:, 1:2], 1.0 / D_SGU, sc[:, 3:4], ALU.mult, ALU.subtract,
                    ["smallc"], ["smallc"])
                act_op(sc[:, 5:6], sc[:, 4:5], AF.Sqrt, ["smallc", "ccol"], ["smallc"], scale=1.0, bias=ccol[:, 0:1])
                P.add("dve", lambda sc=sc: nc.vector.reciprocal(out=sc[:, 6:7], in_=sc[:, 5:6]),
                      reads=["smallc"], writes=["smallc"])
                stt("dve", sc[:, 7:8], sc[:, 2:3], -1.0, sc[:, 6:7], ALU.mult, ALU.mult, ["smallc"], ["smallc"])
                P.small = False
                for hh in range(2):
                    cs_ = slice(hh * 512, (hh + 1) * 512)
                    act_op(j2[:, :], vtok[:, c, cs_], AF.Identity, ["vtok", "smallc"], [j2key],
                           scale=sc[:, 6:7], bias=sc[:, 7:8])
                    tt("dve", vtok[:, c, cs_], j2[:, :], sgu_g[:, l, cs_], ALU.mult, [j2key, "sgu_g"], ["vtok"])

            bss_box = [None]

            def sgu_unit(h):
                if h == 0:
                    bss_box[0] = bank("stat")
                bss = bss_box[0]
                b = bank("mm2")
                def fn(b=b, h=h, l=l):
                    ins = None
                    for c in range(NCH):
                        nc.tensor.matmul(ps[b][:, c * 128:(c + 1) * 128], lhsT=vtok[:, c, h * 128:(h + 1) * 128],
                                         rhs=wmT[:, l, h, :], start=True, stop=False)
                        ins = nc.tensor.matmul(ps[b][:, c * 128:(c + 1) * 128], lhsT=rows_b[0:1, 0:128],
                                               rhs=rows_b[0:1, 128 + l * 1024 + h * 128:128 + l * 1024 + (h + 1) * 128],
                                               start=False, stop=True)
                    return ins
                P.add("pe", fn, reads=["vtok", "wmT", "rows_b"], writes=[PK(b)])
                ya, yakey = tmp()
                tt("dve", ya[:, :], ps[b][:, :], uT[:, h, :], ALU.mult, [PK(b), "uT"], [yakey])
                act_op(mixT[:, h, :], ya[:, :], AF.Identity, [yakey, "gvec"], ["mixT"], scale=gvec[:, l, 48 + h:49 + h])
                sq, sqkey = sqt()
                act_op(sq[:, :], ya[:, :], AF.Square, [yakey], [sqkey])
                P.add("pe", lambda sq=sq, h=h, bss=bss: nc.tensor.matmul(ps[bss][:, :], lhsT=ones_b[:, :], rhs=sq[:, :],
                                                                         start=(h == 0), stop=(h == 7)),
                      reads=[sqkey, "ones_b"], writes=[PK(bss)])
                if h == 7:
                    finish_rstd(bss, rstd_a, "rstd_a", float(D_SGU))

            def s5_A(g):
                mt, j = divmod(g, 8)
                gr, gwkey, gtkey = gload(l, g, t0)
                b1 = bank("s5p"); b2 = bank("s5p")
                P.add("pe", lambda b1=b1, gr=gr, mt=mt: nc.tensor.matmul(
                    ps[b1][:, :], lhsT=gr[:, 0:128], rhs=xsb[:, mt, :], start=True, stop=True),
                    reads=[gwkey, "xsb"], writes=[PK(b1)])
                P.add("pe", lambda b2=b2, gr=gr, mt=mt: nc.tensor.matmul(
                    ps[b2][:, :], lhsT=gr[:, 128:256], rhs=xsb[:, mt, :], start=True, stop=True),
                    reads=[gwkey, "xsb"], writes=[PK(b2)])
                assert b2 == b1 + 1
                pb = s5pb[g % 2]; pbkey = f"s5pb{g % 2}"
                cp("act", pb[:, :], psall[:, b1 * 512:(b1 + 2) * 512], [PK(b1), PK(b2)], [pbkey] + mixw(l, s, ti, pbkey))
                return (g, mt, j, gr, gwkey, gtkey, b1, b2)

            def s5_BD(a, by):
                g, mt, j, gr, gwkey, gtkey, b1, b2 = a
                assert b2 == b1 + 1
                pb = s5pb[g % 2]; pbkey = f"s5pb{g % 2}"
                t12 = s5t12[g % 2]; t12key = f"s5t12{g % 2}"
                G = s5g[g % NS5]; gkey = f"s5g{g % NS5}"
                G12 = s5g12[g % NS5]; g12key = f"s5g12{g % NS5}"
                tab = gr[:, 512:1536]
                tt("dve", t12[:, :], pb[:, :], tab, ALU.mult, [pbkey, gtkey], [t12key] + mixw(l, s, ti, t12key))
                tt("dve", G[:, :], t12[:, 0:512], t12[:, 512:1024], ALU.add, [t12key], [gkey] + mixw(l, s, ti, gkey))
                P.add("dve", lambda G=G, l=l, g=g: nc.vector.tensor_tensor_scan(
                    out=G[:, :], data0=amag[:, l, g:g + 1].broadcast_to([128, NT]), data1=G[:, :],
                    initial=s5c[:, l, g:g + 1], op0=ALU.mult, op1=ALU.add),
                    reads=[gkey, "amag", ("s5c", l, g)], writes=[gkey])
                cp("act", s5c[:, l, g:g + 1], G[:, NT - 1:NT], [gkey], [("s5c", l, g)])
                tt("dve", G12[:, :, :], G[:, :].unsqueeze(1).broadcast_to([128, 2, NT]),
                   tab.rearrange("p (a n) -> p a n", a=2), ALU.mult, [gkey, gtkey], [g12key] + mixw(l, s, ti, g12key))
                return (g, mt, j, gr, gwkey, G12, g12key, by)

            def s5_F(f):
                g, mt, j, gr, gwkey, G12, g12key, by = f
                def fn(by=by, gr=gr, j=j, G12=G12):
                    nc.tensor.matmul(ps[by][:, :], lhsT=gr[:, 256:384], rhs=G12[:, 0, :],
                                     start=(j == 0), stop=False)
                    return nc.tensor.matmul(ps[by][:, :], lhsT=gr[:, 384:512], rhs=G12[:, 1, :],
                                            start=False, stop=(j == 7))
                P.add("pe", fn, reads=[gwkey, g12key], writes=[PK(by)])
                if j == 7:
                    yt, ytkey = tmp()
                    stt("dve", yt[:, :], xsb[:, mt, :], gvec[:, l, 64 + mt:65 + mt], ps[by][:, :], ALU.mult, ALU.add,
                        ["xsb", "gvec", PK(by)], [ytkey])
                    act_op(zf[:, mt, :], yt[:, :], AF.Gelu_apprx_tanh, [ytkey], ["zf"] + mixw(l, s, ti, "zf"))
                    cp("act", zb[:, mt, :], zf[:, mt, :], ["zf"], ["zb"] + mixw(l, s, ti, "zb"))

            def s5_pipeline():
                by = bank("acc")
                a_next = s5_A(0)
                f_prev = None
                for g in range(NG):
                    a_cur = a_next
                    a_next = s5_A(g + 1) if g + 1 < NG else None
                    f_cur = s5_BD(a_cur, by)
                    if f_prev is not None:
                        s5_F(f_prev)
                    f_prev = f_cur
                    yield
                s5_F(f_prev)
                yield

            others = ([lambda m=m: xs_unit(m) for m in range(8)]
                      + [lambda m=m: u_unit(m) for m in range(8)] + [lambda m=m: v_unit(m) for m in range(8)]
                      + [lambda c=c: ln_unit(c) for c in range(NCH)] + [lambda h=h: sgu_unit(h) for h in range(8)])
            gen = s5_pipeline()
            n_steps = NG + 1
            done = 0
            for ui, ufn in enumerate(others):
                ufn()
                target = (n_steps * (ui + 1)) // len(others)
                while done < target:
                    next(gen)
                    done += 1
            for _ in gen:
                pass
            bss = bank("stat")
            for m in range(8):
                wv, wkey = wload(unit("s5_glu_w", l, m), 8 * 128)
                b = bank("mm")
                mm_group(b, wv, wkey, 8, lambda k: zb[:, k, :], ["zb"])
                gt, gtkey = tmp()
                act_op(gt[:, :], ps[b][:, :], AF.Sigmoid, [PK(b), "gvec"], [gtkey], bias=gvec[:, l, 72 + m:73 + m])
                yb, ybkey = tmp()
                tt("dve", yb[:, :], zf[:, m, :], gt[:, :], ALU.mult, ["zf", gtkey], [ybkey])
                act_op(mixT[:, 8 + m, :], yb[:, :], AF.Identity, [ybkey, "gvec"], ["mixT"],
                       scale=gvec[:, l, 56 + m:57 + m])
                sq, sqkey = sqt()
                act_op(sq[:, :], yb[:, :], AF.Square, [ybkey], [sqkey])
                P.add("pe", lambda sq=sq, m=m, bss=bss: nc.tensor.matmul(ps[bss][:, :], lhsT=ones_b[:, :], rhs=sq[:, :],
                                                                         start=(m == 0), stop=(m == 7)),
                      reads=[sqkey, "ones_b"], writes=[PK(bss)])
            finish_rstd(bss, rstd_b, "rstd_b", float(D_SSM))
            sb_next = bank("stat")
            for m in range(KT_D):
                wv, wkey = wload(unit("w_out", l, m), KT_D * 128)
                ba = bank("mm"); bb = bank("mm")
                def fn(ba=ba, bb=bb, wv=wv):
                    ins = None
                    for k in range(8):
                        nc.tensor.matmul(ps[ba][:, :], lhsT=wv[:, k * 128:(k + 1) * 128], rhs=mixT[:, k, :],
                                         start=(k == 0), stop=(k == 7))
                    for k in range(8, 16):
                        ins = nc.tensor.matmul(ps[bb][:, :], lhsT=wv[:, k * 128:(k + 1) * 128], rhs=mixT[:, k, :],
                                               start=(k == 8), stop=(k == 15))
                    return ins
                P.add("pe", fn, reads=[wkey, "mixT"], writes=[PK(ba), PK(bb)])
                ta, takey = tmp()
                tt("dve", ta[:, :], ps[ba][:, :], rstd_a[:, :], ALU.mult, [PK(ba), "rstd_a"], [takey])
                if "nosgu" not in debug:
                    tt("pool", xT[:, m, :], xT[:, m, :], ta[:, :], ALU.add, [("xT", m), takey], [("xT", m)])
                tb2, tb2key = tmp()
                tt("dve", tb2[:, :], ps[bb][:, :], rstd_b[:, :], ALU.mult, [PK(bb), "rstd_b"], [tb2key])
                if "nos5" not in debug:
                    tt("pool", xT[:, m, :], xT[:, m, :], tb2[:, :], ALU.add, [("xT", m), tb2key], [("xT", m)])
                stat_lagged(sb_next, m)

            if "dump" in debug and l == 0 and s == 0 and ti == 0:
                dumps = {"mixT": (mixT, BF16, 8192, ["mixT"]), "zf": (zf, F32, 4096, ["zf"]),
                         "vtok": (vtok, BF16, 4096, ["vtok"]), "uT": (uT, BF16, 4096, ["uT"]),
                         "xsb": (xsb, BF16, 4096, ["xsb"]), "rstd_a": (rstd_a, F32, 512, ["rstd_a"]),
                         "rstd_b": (rstd_b, F32, 512, ["rstd_b"]), "zb": (zb, BF16, 4096, ["zb"])}
                for dn, (buf, dt_, n_, keys_) in dumps.items():
                    dd = nc.dram_tensor("dbg_" + dn, [128, n_], dt_, kind="ExternalOutput").ap()
                    src_ = buf if len(buf.shape) == 2 else buf.rearrange("p a b -> p (a b)")
                    dma("pool", dd[:, :], src_, keys_, ["OUT"], ("dbg", dn))
            norm_to_hT(l, 16, sb_next)
            NJ = D_FF // 128
            for j in range(NJ):
                wg, wgkey = wload(unit("ffn_w_up", l, j), KT_D * 128)
                wu, wukey = wload(unit("ffn_w_up", l, NJ + j), KT_D * 128)
                bg = bank("mm"); bu = bank("mm")
                mm_group(bg, wg, wgkey, KT_D, lambda k: hT[:, k, :], ["hT"])
                mm_group(bu, wu, wukey, KT_D, lambda k: hT[:, k, :], ["hT"])
                cres = []
                for half, (bx, jj) in enumerate(((bg, j), (bu, NJ + j))):
                    ab = abuf[(2 * j + half) % 4]; abkey = f"ab{(2 * j + half) % 4}"
                    first = (j < 2)
                    cp("pool", ab[:, 0:2], cc[:, l, jj, :], ["cc"],
                       [abkey] + (UNION_KEYS_MIX if first else []))
                    cp("act", ab[:, 2:2 + NT], ps[bx][:, :], [PK(bx)], [abkey])
                    cp("pool", cc[:, l, jj, :], ab[:, NT:NT + 2], [abkey], ["cc"])
                    ct, ctkey = tmp()
                    act_op(ct[:, :], ps[bx][:, :], AF.Identity, [PK(bx), "convc"], [ctkey],
                           scale=convc[:, l, 2, jj:jj + 1], bias=convc[:, l, 3, jj:jj + 1])
                    stt("dve", ct[:, :], ab[:, 1:1 + NT], convc[:, l, 1, jj:jj + 1], ct[:, :], ALU.mult, ALU.add,
                        [abkey, "convc", ctkey], [ctkey])
                    stt("dve", ct[:, :], ab[:, 0:NT], convc[:, l, 0, jj:jj + 1], ct[:, :], ALU.mult, ALU.add,
                        [abkey, "convc", ctkey], [ctkey])
                    cres.append((ct, ctkey))
                (cg, cgkey), (cu, cukey) = cres
                act_op(cg[:, :], cg[:, :], AF.Silu, [cgkey], [cgkey])
                tt("dve", hid[:, j, :], cg[:, :], cu[:, :], ALU.mult, [cgkey, cukey],
                   ["hid"] + (UNION_KEYS_MIX if j == 0 else []))
            sb_next = bank("stat")
            for m in range(KT_D):
                w0, w0key = wload(unit("ffn_w_down", l, m)[:, 0:22 * 128], 22 * 128)
                w1, w1key = wload(unit("ffn_w_down", l, m)[:, 22 * 128:44 * 128], 22 * 128)
                b = bank("mm")
                def fn(b=b, w0=w0, w1=w1):
                    ins = None
                    for k in range(44):
                        wv = w0 if k < 22 else w1
                        kk = k % 22
                        ins = nc.tensor.matmul(ps[b][:, :], lhsT=wv[:, kk * 128:(kk + 1) * 128], rhs=hid[:, k, :],
                                               start=(k == 0), stop=(k == 43))
                    return ins
                P.add("pe", fn, reads=[w0key, w1key, "hid"], writes=[PK(b)])
                if "noffn" not in debug:
                    tt("dve", xT[:, m, :], xT[:, m, :], ps[b][:, :], ALU.add, [("xT", m), PK(b)], [("xT", m)])
                else:
                    cp("dve", tmpf[0][:, :], ps[b][:, :], [PK(b)], ["tmp0"])
                stat_lagged(sb_next, m)

            norm_to_hT(l, 32, sb_next)
            for c in range(NCH):
                dma("pool", ptok[:, c, :], p_d[l, s, t0 + c * 128:t0 + (c + 1) * 128, :], [],
                    ["ptok"] + (UNION_KEYS_MIX if c == 0 else []), ("pt", 0))
            for kk in range(2):
                b = bank("mm")
                def fn(b=b, kk=kk):
                    ins = None
                    for c in range(NCH):
                        ins = nc.tensor.transpose(out=ps[b][:, c * 128:(c + 1) * 128],
                                                  in_=ptok[:, c, kk * 128:(kk + 1) * 128], identity=ident[:, :])
                    return ins
                P.add("pe", fn, reads=["ptok", "ident"], writes=[PK(b)])
                cp("dve", pT[:, kk, :], ps[b][:, :], [PK(b)], ["pT"] + (UNION_KEYS_MIX if kk == 0 else []))
            sb_next = bank("stat")
            for m in range(KT_D):
                wg, wgkey = wload(unit("ple_w_gate", l, m), KT_D * 128)
                wp, wpkey = wload(unit("ple_w_proj", l, m), 2 * 128)
                bg = bank("mm"); bp = bank("mm")
                mm_group(bg, wg, wgkey, KT_D, lambda k: hT[:, k, :], ["hT"])
                mm_group(bp, wp, wpkey, 2, lambda k: pT[:, k, :], ["pT"])
                gt, gtkey = tmp()
                act_op(gt[:, :], ps[bg][:, :], AF.Sigmoid, [PK(bg)], [gtkey])
                tt("dve", gt[:, :], gt[:, :], ps[bp][:, :], ALU.mult, [gtkey, PK(bp)], [gtkey])
                if "nople" not in debug:
                    tt("pool", xT[:, m, :], xT[:, m, :], gt[:, :], ALU.add, [("xT", m), gtkey], [("xT", m)])
                stat_lagged(sb_next, m)
            sb_mix = sb_next

        finish_rstd(sb_mix, rstd_x, "rstd_x", float(D))
        for k in range(KT_D):
            stt("dve", xT[:, k, :], xT[:, k, :], fin_g[:, k:k + 1], rstd_x[:, :], ALU.mult, ALU.mult,
                [("xT", k), "fin_g", "rstd_x"], [("xT", k)])
        for c in range(NCH):
            st = iostage[c % 2]; skey = f"io{c % 2}"
            for kq in range(4):
                b = bank("mm")
                def fn(b=b, kq=kq, c=c):
                    ins = None
                    for kk in range(4):
                        k = kq * 4 + kk
                        ins = nc.tensor.transpose(out=ps[b][:, kk * 128:(kk + 1) * 128],
                                                  in_=xT[:, k, c * 128:(c + 1) * 128], identity=ident[:, :])
                    return ins
                P.add("pe", fn, reads=[("xT", kq * 4 + kk_) for kk_ in range(4)] + ["ident"], writes=[PK(b)])
                cp("act" if kq % 2 else "dve", st[:, kq * 512:(kq + 1) * 512], ps[b][:, :], [PK(b)],
                   [skey] + ((UNION_KEYS_MIX + UNION_KEYS_FFN) if (c < 2 and kq == 0) else []))
            dma("pool", out_d[s, t0 + c * 128:t0 + (c + 1) * 128, :], st[:, :], [skey], ["OUT"], ("io", c % 2))

    P.add("pool", lambda: None, reads=[f"io0", "io1"], writes=["OUT", "io0", "io1", "mixT", "zf", "vtok", "uT", "xsb", "rstd_a", "rstd_b", "zb"])

    P.finalize_counts()
    n_dma = len(P.dma_keys)
    sem_cms = [nc.semaphore(f"e_{e}") for e in ENGS] + [nc.semaphore(f"d_{i}") for i in range(n_dma)]
    sems = [cm.__enter__() for cm in sem_cms]
    eng_sems = {e: sems[i] for i, e in enumerate(ENGS)}
    dma_sems = {i: sems[len(ENGS) + i] for i in range(n_dma)}
    with nc.Block() as block:
        @block.sync
        def _(e):
            P.emit_engine("sp", nc.sync, eng_sems, dma_sems)

        @block.scalar
        def _(e):
            P.emit_engine("act", nc.scalar, eng_sems, dma_sems)

        @block.vector
        def _(e):
            P.emit_engine("dve", nc.vector, eng_sems, dma_sems)

        @block.gpsimd
        def _(e):
            P.emit_engine("pool", nc.gpsimd, eng_sems, dma_sems)

        @block.tensor
        def _(e):
            P.emit_engine("pe", nc.tensor, eng_sems, dma_sems)
    for cm in reversed(sem_cms):
        cm.__exit__(None, None, None)
    for cm in reversed(ps_cms):
        cm.__exit__(None, None, None)
    arena_cm.__exit__(None, None, None)
    return nc


_WNAMES = ["mix_norm", "w_in", "sgu_norm", "sgu_w", "sgu_b", "s5_lam_re", "s5_lam_im", "s5_log_dt",
           "s5_b_re", "s5_b_im", "s5_c_re", "s5_c_im", "s5_d", "s5_glu_w", "s5_glu_b", "out_norm_a",
           "out_norm_b", "w_out", "ffn_norm", "ffn_w_up", "ffn_conv_w", "ffn_conv_b", "ffn_w_down",
           "ple_norm", "ple_w_gate", "ple_w_proj", "final_norm"]


def kernel(**inputs):
    n_cores = 8
    per = NB // n_cores
    nc = build_program()
    shared = {k: np.ascontiguousarray(np.asarray(inputs[k], dtype=np.float32)) for k in _WNAMES}
    x = np.asarray(inputs["x"], dtype=np.float32)
    p = np.asarray(inputs["p"], dtype=np.float32)
    in_maps = []
    for c in range(n_cores):
        m = dict(shared)
        m["x"] = np.ascontiguousarray(x[c * per:(c + 1) * per])
        m["p"] = np.ascontiguousarray(p[:, c * per:(c + 1) * per])
        in_maps.append(m)
    res = run_bass_kernel_spmd(nc, in_maps, core_ids=list(range(n_cores)))
    out = np.concatenate([np.asarray(r["out"]) for r in res.results], axis=0)
    return out.astype(np.float32, copy=False)
```
